# Optimizing a Trainium2 kernel written in Bass

```python
import math
import jax
import jax.numpy as jnp
from jax import lax
import numpy as np

D_MODEL = 1024
BATCH = 8
SEQ = 4096
DEPTH = 2

N_MIXERS = 2
HEAD_DIM = 64
MIX_WIDTH = 3 * D_MODEL // 4
MEM_WIDTH = D_MODEL - MIX_WIDTH
MEM_HEADS = 4
MEM_HEAD_DIM = MEM_WIDTH // MEM_HEADS
MEM_TOKENS = 256
RWKV_HEADS = MIX_WIDTH // HEAD_DIM
DECAY_LORA = 64
ICLR_LORA = 64
GATE_LORA = 128
RWKV_COLS = 3 * MIX_WIDTH + DECAY_LORA + ICLR_LORA + GATE_LORA
RWKV_IN = RWKV_COLS + MEM_WIDTH
DIFF_HEADS = MIX_WIDTH // (2 * HEAD_DIM)
DIFF_IN = 3 * MIX_WIDTH + MEM_WIDTH
D_FF = 4 * D_MODEL
N_RWKV_LAYERS = (DEPTH + N_MIXERS - 1) // N_MIXERS
N_DIFF_LAYERS = DEPTH // N_MIXERS
Q_BLOCK = 128
RMS_EPS = 1e-6
GN_EPS = 64e-5

kernel_name = "rwkv7_diffattn_alibi_memxattn_hybrid"


def rms_norm(x, g=None):
    xf = x.astype(jnp.float32)
    y = xf * lax.rsqrt(jnp.mean(jnp.square(xf), axis=-1, keepdims=True) + RMS_EPS)
    if g is not None:
        y = y * g.astype(jnp.float32)
    return y.astype(x.dtype)


def token_shift(z, mu):
    z_prev = jnp.pad(z[:, :-1], ((0, 0), (1, 0), (0, 0)))
    return z + (z_prev - z) * mu


def delta_rule_scan(r, w, k, v, a, b):
    B, T, H, N = r.shape
    xs = tuple(jnp.moveaxis(z.astype(jnp.float32), 1, 0) for z in (r, w, k, v, a, b))

    def step(S, inp):
        r_t, w_t, k_t, v_t, a_t, b_t = inp
        sa = jnp.einsum('bhvk,bhk->bhv', S, a_t)
        S = (S * w_t[:, :, None, :] + sa[..., None] * b_t[:, :, None, :]
             + v_t[..., None] * k_t[:, :, None, :])
        return S, jnp.einsum('bhvk,bhk->bhv', S, r_t)

    S0 = jnp.zeros((B, H, N, N), jnp.float32)
    _, y = lax.scan(step, S0, xs)
    return jnp.moveaxis(y, 0, 1)


def rwkv7_time_mix(h, w_in, mu, w0, w2, a0, a2, g2, k_k, k_a, r_k, lnx_g, lnx_b):
    B, T, _ = h.shape
    H, N = RWKV_HEADS, HEAD_DIM
    proj = h @ w_in
    slab = token_shift(proj[..., :RWKV_COLS], mu)
    q_mem = proj[..., RWKV_COLS:]
    c3 = 3 * MIX_WIDTH
    r, k, v, wd, ad, gd = jnp.split(
        slab, [MIX_WIDTH, 2 * MIX_WIDTH, c3, c3 + DECAY_LORA, c3 + DECAY_LORA + ICLR_LORA], axis=-1)
    log_w = -jax.nn.softplus(-(w0 + jnp.tanh(wd) @ w2)) - 0.5
    decay = jnp.exp(-jnp.exp(log_w.astype(jnp.float32)))
    a = jax.nn.sigmoid(a0 + ad @ a2)
    g = jax.nn.sigmoid(gd) @ g2

    def heads(z):
        return z.reshape(B, T, H, N)

    kk = heads(k * k_k).astype(jnp.float32)
    kk = kk / jnp.maximum(jnp.sqrt(jnp.sum(jnp.square(kk), axis=-1, keepdims=True)), 1e-12)
    k = k * (1.0 + (a - 1.0) * k_a)
    a_h = heads(a).astype(jnp.float32)
    r_h, k_h, v_h = heads(r), heads(k), heads(v)
    y = delta_rule_scan(r_h, heads(decay), k_h, v_h, -kk, kk * a_h)
    mean = jnp.mean(y, axis=-1, keepdims=True)
    var = jnp.mean(jnp.square(y - mean), axis=-1, keepdims=True)
    y = ((y - mean) * lax.rsqrt(var + GN_EPS)).reshape(B, T, MIX_WIDTH)
    y = (y * lnx_g.astype(jnp.float32) + lnx_b.astype(jnp.float32)).astype(h.dtype)
    bonus = (jnp.sum(r_h * k_h * r_k, axis=-1, keepdims=True) * v_h).reshape(B, T, MIX_WIDTH)
    return (y + bonus) * g, q_mem


def alibi_slopes(n):
    return jnp.exp2(-8.0 * jnp.arange(1, n + 1, dtype=jnp.float32) / n)


def causal_diff_attention(q, k, v, lam, slopes):
    B, T, H, _, d = q.shape
    nb = T // Q_BLOCK
    qb = jnp.moveaxis(q.reshape(B, nb, Q_BLOCK, H, 2, d), 1, 0)
    starts = jnp.arange(nb, dtype=jnp.int32) * Q_BLOCK
    k_pos = jnp.arange(T, dtype=jnp.int32)
    scale = d ** -0.5

    def block(args):
        q_blk, start = args
        q_pos = start + jnp.arange(Q_BLOCK, dtype=jnp.int32)
        dist = q_pos[:, None] - k_pos[None, :]
        bias = -slopes[:, None, None] * dist.astype(jnp.float32)[None]
        s = jnp.einsum('bqhcd,bkhcd->bhcqk', q_blk, k).astype(jnp.float32) * scale
        s = jnp.where(dist >= 0, s + bias[None, :, None], -jnp.inf)
        p = jax.nn.softmax(s, axis=-1)
        p = p[:, :, 0] - lam * p[:, :, 1]
        return jnp.einsum('bhqk,bkhe->bqhe', p.astype(v.dtype), v)

    out = lax.map(block, (qb, starts))
    return jnp.moveaxis(out, 0, 1).reshape(B, T, H, 2 * d)


def diff_lambda_init(layer):
    return 0.8 - 0.6 * math.exp(-0.3 * layer)


def diff_attention_mixer(h, w_in, q_g, k_g, lq1, lk1, lq2, lk2, subln_g, lambda_init):
    B, T, _ = h.shape
    H, d = DIFF_HEADS, HEAD_DIM
    proj = h @ w_in
    q = rms_norm(proj[..., :MIX_WIDTH].reshape(B, T, H, 2, d), q_g)
    k = rms_norm(proj[..., MIX_WIDTH:2 * MIX_WIDTH].reshape(B, T, H, 2, d), k_g)
    v = proj[..., 2 * MIX_WIDTH:3 * MIX_WIDTH].reshape(B, T, H, 2 * d)
    q_mem = proj[..., 3 * MIX_WIDTH:]
    lam = (jnp.exp(jnp.sum(lq1 * lk1).astype(jnp.float32))
           - jnp.exp(jnp.sum(lq2 * lk2).astype(jnp.float32)) + lambda_init)
    o = causal_diff_attention(q, k, v, lam, alibi_slopes(H))
    o = rms_norm(o, subln_g) * (1.0 - lambda_init)
    return o.reshape(B, T, MIX_WIDTH), q_mem


def memory_cross_attention(q_mem, k_mem, v_mem, q_g, k_g):
    B, T, _ = q_mem.shape
    q = rms_norm(q_mem.reshape(B, T, MEM_HEADS, MEM_HEAD_DIM), q_g)
    k = k_mem * k_g
    s = jnp.einsum('bthd,bmhd->bhtm', q, k).astype(jnp.float32) * MEM_HEAD_DIM ** -0.5
    p = jax.nn.softmax(s, axis=-1).astype(v_mem.dtype)
    return jnp.einsum('bhtm,bmhd->bthd', p, v_mem).reshape(B, T, MEM_WIDTH)


def setup_inputs(seed: int = 0) -> dict:
    key = jax.random.key(seed)
    keys = jax.random.split(key, 40)
    counter = [0]

    def next_key():
        counter[0] += 1
        return keys[counter[0] - 1]

    def nrm(shape, scale):
        return jax.random.normal(next_key(), shape, jnp.float32) * scale

    def gain(shape):
        return 1.0 + nrm(shape, 0.05)

    def unif(shape, lo, hi):
        return jax.random.uniform(next_key(), shape, jnp.float32, lo, hi)

    NR, ND = N_RWKV_LAYERS, N_DIFF_LAYERS
    return {
        "x": nrm((BATCH, SEQ, D_MODEL), 1.0),
        "mem": nrm((BATCH, MEM_TOKENS, D_MODEL), 1.0),
        "norm_mix_g": gain((DEPTH, D_MODEL)),
        "norm_ffn_g": gain((DEPTH, D_MODEL)),
        "w_out": nrm((DEPTH, D_MODEL, D_MODEL), D_MODEL ** -0.5),
        "w_ff1": nrm((DEPTH, D_MODEL, D_FF), D_MODEL ** -0.5),
        "w_ff2": nrm((DEPTH, D_FF, D_MODEL), D_FF ** -0.5),
        "mem_norm_g": gain((D_MODEL,)),
        "w_mem_kv": nrm((D_MODEL, 2 * MEM_WIDTH), D_MODEL ** -0.5),
        "mem_q_norm_g": gain((DEPTH, MEM_HEAD_DIM)),
        "mem_k_norm_g": gain((DEPTH, MEM_HEAD_DIM)),
        "rw_in": nrm((NR, D_MODEL, RWKV_IN), D_MODEL ** -0.5),
        "rw_mu": unif((NR, RWKV_COLS), 0.0, 1.0),
        "rw_w0": unif((NR, MIX_WIDTH), -6.0, -1.0),
        "rw_w2": nrm((NR, DECAY_LORA, MIX_WIDTH), 0.5 * DECAY_LORA ** -0.5),
        "rw_a0": nrm((NR, MIX_WIDTH), 0.1),
        "rw_a2": nrm((NR, ICLR_LORA, MIX_WIDTH), ICLR_LORA ** -0.5),
        "rw_g2": nrm((NR, GATE_LORA, MIX_WIDTH), GATE_LORA ** -0.5),
        "rw_k_k": 0.85 + nrm((NR, MIX_WIDTH), 0.05),
        "rw_k_a": gain((NR, MIX_WIDTH)),
        "rw_r_k": nrm((NR, RWKV_HEADS, HEAD_DIM), 0.1),
        "rw_lnx_g": gain((NR, MIX_WIDTH)),
        "rw_lnx_b": nrm((NR, MIX_WIDTH), 0.01),
        "df_in": nrm((ND, D_MODEL, DIFF_IN), D_MODEL ** -0.5),
        "df_q_norm_g": gain((ND, 2, HEAD_DIM)),
        "df_k_norm_g": gain((ND, 2, HEAD_DIM)),
        "df_lq1": nrm((ND, HEAD_DIM), 0.1),
        "df_lk1": nrm((ND, HEAD_DIM), 0.1),
        "df_lq2": nrm((ND, HEAD_DIM), 0.1),
        "df_lk2": nrm((ND, HEAD_DIM), 0.1),
        "df_subln_g": gain((ND, 2 * HEAD_DIM)),
    }


def reference(x, mem, norm_mix_g, norm_ffn_g, w_out, w_ff1, w_ff2, mem_norm_g, w_mem_kv,
              mem_q_norm_g, mem_k_norm_g, rw_in, rw_mu, rw_w0, rw_w2, rw_a0, rw_a2, rw_g2,
              rw_k_k, rw_k_a, rw_r_k, rw_lnx_g, rw_lnx_b, df_in, df_q_norm_g, df_k_norm_g,
              df_lq1, df_lk1, df_lq2, df_lk2, df_subln_g):
    B, M, _ = mem.shape
    kv = rms_norm(mem, mem_norm_g) @ w_mem_kv
    k_mem = rms_norm(kv[..., :MEM_WIDTH].reshape(B, M, MEM_HEADS, MEM_HEAD_DIM))
    v_mem = kv[..., MEM_WIDTH:].reshape(B, M, MEM_HEADS, MEM_HEAD_DIM)

    for layer in range(DEPTH):
        j = layer // N_MIXERS
        h = rms_norm(x, norm_mix_g[layer])
        if layer % N_MIXERS == 0:
            mix, q_mem = rwkv7_time_mix(h, rw_in[j], rw_mu[j], rw_w0[j], rw_w2[j], rw_a0[j],
                                        rw_a2[j], rw_g2[j], rw_k_k[j], rw_k_a[j], rw_r_k[j],
                                        rw_lnx_g[j], rw_lnx_b[j])
        else:
            mix, q_mem = diff_attention_mixer(h, df_in[j], df_q_norm_g[j], df_k_norm_g[j],
                                              df_lq1[j], df_lk1[j], df_lq2[j], df_lk2[j],
                                              df_subln_g[j], diff_lambda_init(layer))
        mem_out = memory_cross_attention(q_mem, k_mem, v_mem, mem_q_norm_g[layer],
                                         mem_k_norm_g[layer])
        x = x + jnp.concatenate([mix, mem_out], axis=-1) @ w_out[layer]
        u = rms_norm(x, norm_ffn_g[layer]) @ w_ff1[layer]
        x = x + jnp.square(jax.nn.relu(u)) @ w_ff2[layer]
    return x
```

```python
import numpy as np
import concourse.bass as bass
import concourse.mybir as mybir
from concourse.bass_utils import run_bass_kernel_spmd

F32 = mybir.dt.float32
BF16 = mybir.dt.bfloat16
AF = mybir.ActivationFunctionType
ALU = mybir.AluOpType
AX = mybir.AxisListType


class Prog:
    NDMA_SEM = 6

    def __init__(self, nc, same_engine_sync=('pool', 'act', 'dve')):
        self.nc = nc
        self.engs = {'pe': nc.tensor, 'act': nc.scalar, 'dve': nc.vector, 'pool': nc.gpsimd, 'sp': nc.sync}
        self.stack = []
        self.same_engine_sync = same_engine_sync
        self.csem = {}
        for e in ('pe', 'act', 'dve', 'pool'):
            self.csem[e] = self._enter(nc.semaphore("c_" + e))
        self.ccount = {e: 0 for e in self.csem}
        self.dsem = {}
        self.dval = {}
        self.dnext = {}
        for q in ('sp', 'pool', 'act'):
            self.dsem[q] = [self._enter(nc.semaphore("d_%s_%d" % (q, i))) for i in range(self.NDMA_SEM)]
            self.dval[q] = [0] * self.NDMA_SEM
            self.dnext[q] = 0
        self.waited = {e: {} for e in self.engs}
        self.last_write = {}
        self.reads_since = {}
        self.final_tickets = []
        self.ninstr = {e: 0 for e in self.engs}
        self.scopes = []
        self.excl = set()
        self.last_rt = None
        self.uid = 0

    def _enter(self, cm):
        v = cm.__enter__()
        self.stack.append(cm)
        return v

    def sb(self, name, shape, dtype):
        self.uid += 1
        return self._enter(self.nc.sbuf_tensor("%s_%d" % (name, self.uid), shape, dtype))

    def ps(self, name, shape, dtype):
        self.uid += 1
        return self._enter(self.nc.psum_tensor("%s_%d" % (name, self.uid), shape, dtype))

    def _wait(self, eng, ticket):
        semkey, sem, val, src = ticket
        if src == eng and eng not in self.same_engine_sync:
            return
        w = self.waited[eng]
        if w.get(semkey, 0) >= val:
            return
        w[semkey] = val
        self.engs[eng].wait_ge(sem, val)
        self.ninstr[eng] += 1

    def _deps(self, eng, reads, writes):
        for k in reads:
            t = self.last_write.get(k)
            if t is not None:
                self._wait(eng, t)
            if k in self.excl:
                for t in self.reads_since.get(k, ()):
                    if t[3] != eng:
                        self._wait(eng, t)
        for k in writes:
            t = self.last_write.get(k)
            if t is not None:
                self._wait(eng, t)
            for t in self.reads_since.get(k, ()):
                self._wait(eng, t)

    def _record(self, ticket, reads, writes):
        for k in reads:
            self.reads_since.setdefault(k, []).append(ticket)
        for k in writes:
            self.last_write[k] = ticket
            self.reads_since[k] = []

    def op(self, eng, fn, reads=(), writes=(), rt=None):
        self._deps(eng, reads, writes)
        if eng == 'pe':
            if rt != self.last_rt and self.ccount['pe'] > 0:
                w = self.waited['pe']
                if w.get('c_pe', 0) < self.ccount['pe']:
                    w['c_pe'] = self.ccount['pe']
                    self.engs['pe'].wait_ge(self.csem['pe'], self.ccount['pe'])
            self.last_rt = rt
        ins = fn(self.engs[eng])
        self.ccount[eng] += 1
        ins.then_inc(self.csem[eng], 1)
        self.ninstr[eng] += 1
        t = ('c_' + eng, self.csem[eng], self.ccount[eng], eng)
        self._record(t, reads, writes)
        return t

    def dma(self, q, out, in_, reads=(), writes=(), out_final=False, **kw):
        self._deps(q, reads, writes)
        i = self.dnext[q]
        self.dnext[q] = (i + 1) % self.NDMA_SEM
        sem = self.dsem[q][i]
        semkey = 'd_%s_%d' % (q, i)
        if self.dval[q][i] > 0:
            self._wait(q, (semkey, sem, self.dval[q][i], None))
        self.dval[q][i] += 16
        self.engs[q].dma_start(out=out, in_=in_, **kw).then_inc(sem, 16)
        self.ninstr[q] += 1
        t = (semkey, sem, self.dval[q][i], None)
        self._record(t, reads, writes)
        if out_final:
            self.final_tickets.append(t)
        return t

    def scope_begin(self):
        self.scopes.append(len(self.stack))

    def scope_end(self):
        self.barrier()
        n = self.scopes.pop()
        while len(self.stack) > n:
            self.stack.pop().__exit__(None, None, None)

    def barrier(self):
        for e in self.engs:
            for f in self.csem:
                if f != e and self.ccount[f] > 0:
                    self._wait(e, ('c_' + f, self.csem[f], self.ccount[f], f))
            for q in self.dsem:
                for i in range(self.NDMA_SEM):
                    if self.dval[q][i] > 0:
                        self._wait(e, ('d_%s_%d' % (q, i), self.dsem[q][i], self.dval[q][i], None))
        self.last_write = {}
        self.reads_since = {}

    def finish(self):
        for t in self.final_tickets:
            self._wait('sp', t)
        while self.stack:
            self.stack.pop().__exit__(None, None, None)


D = 1024
T = 4096
DFF = 4096
NCORES = 8
RMS_EPS = 1e-6


class Rot:
    def __init__(self, p, name, shape, dtype, n, psum=False):
        mk = p.ps if psum else p.sb
        self.tiles = [mk("%s%d" % (name, i), shape, dtype) for i in range(n)]
        self.keys = [(name, i) for i in range(n)]
        if psum:
            p.excl.update(self.keys)
        self.i = 0

    def next(self):
        t, k = self.tiles[self.i], self.keys[self.i]
        self.i = (self.i + 1) % len(self.tiles)
        return t, k


def make_consts(p):
    c = {}
    ones = p.sb('c_ones', [128, 128], F32)
    p.op('pool', lambda e: e.memset(ones[:], 1.0), writes=['c_ones'])
    idf = p.sb('c_idf', [128, 128], F32)
    p.op('pool', lambda e: e.affine_select(out=idf[:], in_=ones[:], pattern=[[-1, 128]],
                                           compare_op=ALU.is_equal, fill=0.0, base=0, channel_multiplier=1),
         reads=['c_ones'], writes=['c_idf'])
    idb = p.sb('c_idb', [128, 128], BF16)
    p.op('pool', lambda e: e.tensor_copy(idb[:], idf[:]), reads=['c_idf'], writes=['c_idb'])
    c['ones'], c['idf'], c['idb'] = ones, idf, idb
    return c


class NormT:
    def __init__(self, p, c, name, tp=None, nxn=1):
        self.p, self.c = p, c
        self.ss = Rot(p, name + '_ss', [128, 1], F32, 4)
        self.rs = Rot(p, name + '_rs', [128, 1], F32, 4)
        self.xn = Rot(p, name + '_xn', [128, D], BF16, nxn)
        self.tp = tp if tp is not None else Rot(p, name + '_tp', [128, 8, 128], BF16, 2, psum=True)

    def emit(self, x_ap, xkey, gb, gkey, hT, hkeys, col0, evac_eng='act'):
        st = self.emit_a(x_ap, xkey, gb, gkey)
        self.emit_b(st, hT, hkeys, col0, evac_eng)

    def emit_a(self, x_ap, xkey, gb, gkey):
        p, c = self.p, self.c
        ss, ssk = self.ss.next()
        rs, rsk = self.rs.next()
        xn, xnk = self.xn.next()
        p.op('act', lambda e: e.activation(xn[:], x_ap, AF.Square, accum_out=ss[:]),
             reads=[xkey], writes=[xnk, ssk])
        p.op('act', lambda e: e.activation(rs[:], ss[:], AF.Sqrt, bias=c['eps_rms'][:], scale=1.0 / D),
             reads=[ssk], writes=[rsk])
        p.op('dve', lambda e: e.reciprocal(rs[:], rs[:]), reads=[rsk], writes=[rsk])
        p.op('dve', lambda e: e.scalar_tensor_tensor(xn[:], x_ap, rs[:], gb, ALU.mult, ALU.mult),
             reads=[xkey, rsk, gkey], writes=[xnk])
        return xn, xnk

    def emit_b(self, st, hT, hkeys, col0, evac_eng='act'):
        p, c = self.p, self.c
        xn, xnk = st
        tp, tpk = self.tp.next()
        for k in range(8):
            p.op('pe', lambda e: e.transpose(tp[:, k, :], xn[:, k * 128:(k + 1) * 128], c['idb'][:]),
                 reads=[xnk, 'c_idb'], writes=[tpk])
        if evac_eng == 'act':
            p.op('act', lambda e: e.activation(hT[:, :, col0:col0 + 128], tp[:], AF.Copy),
                 reads=[tpk], writes=hkeys)
        else:
            p.op(evac_eng, lambda e: e.tensor_copy(hT[:, :, col0:col0 + 128], tp[:]),
                 reads=[tpk], writes=hkeys)


def load_w_bf16(p, dst, dkey, src2d, nk, ncols, split=None):
    keys = []
    for k in range(nk):
        key = (dkey, k)
        p.dma('pool', dst[:, k, :], src2d[k * 128:(k + 1) * 128, :], writes=[key], max_dma_last_dim=8192)
        keys.append(key)
    return keys


def ffn_phase(p, c, x_src, x_dst, w1, w2, gvec, is_final):
    G = 256
    NS = G // 128
    NG = T // G
    p.scope_begin()
    W1 = p.sb('W1', [128, 8, DFF], BF16)
    W2 = p.sb('W2', [128, 32, D], BF16)
    gb = p.sb('f_gb', [128, D], F32)
    p.dma('sp', gb[:], gvec.partition_broadcast(128), writes=['f_gb'])
    w1keys = load_w_bf16(p, W1, 'W1', w1, 8, DFF)
    w2keys = load_w_bf16(p, W2, 'W2', w2, 32, D)
    xg = Rot(p, 'f_xg', [128, NS, D], F32, 2)
    hTr = Rot(p, 'f_hT', [128, 8, G], BF16, 2)
    hid = p.sb('f_hid', [128, 32, G], BF16)
    sq = Rot(p, 'f_sq', [128, G], F32, 3)
    ups = Rot(p, 'f_ups', [128, G], F32, 3, psum=True)
    yps = Rot(p, 'f_yps', [128, 512], F32, 2, psum=True)
    nt = NormT(p, c, 'f_nt', nxn=NS)
    xs = x_src.rearrange("(n s q) d -> n q s d", s=NS, q=128)
    xd = x_dst.rearrange("(n s q) d -> n q s d", s=NS, q=128)

    def load_norm(g):
        xt, xk = xg.next()
        hT, hk = hTr.next()
        p.dma('sp', xt[:], xs[g], writes=[xk], reads=[('xsrc', g)])
        sts = [nt.emit_a(xt[:, s, :], xk, gb[:], 'f_gb') for s in range(NS)]
        return xt, xk, hT, hk, sts

    def transposes(nx):
        xt, xk, hT, hk, sts = nx
        for s in range(NS):
            nt.emit_b(sts[s], hT, [hk], s * 128)

    nxt = load_norm(0)
    transposes(nxt)
    for g in range(NG):
        xt, xk, hT, hk, _ = nxt
        for f in range(32):
            u, uk = ups.next()
            for k in range(8):
                p.op('pe', lambda e: e.matmul(u[:], W1[:, k, f * 128:(f + 1) * 128], hT[:, k, :],
                                              start=(k == 0), stop=(k == 7)),
                     reads=[('W1', k), hk], writes=[uk])
            s2, sk = sq.next()
            p.op('act', lambda e: e.activation(s2[:], u[:], AF.Square), reads=[uk], writes=[sk])
            p.op('dve', lambda e: e.scalar_tensor_tensor(hid[:, f, :], u[:], 0.0, s2[:], ALU.is_gt, ALU.mult),
                 reads=[uk, sk], writes=[('f_hid', f)])
        if g + 1 < NG:
            nxt = load_norm(g + 1)
        for s in range(NS):
            for h in range(2):
                y, yk = yps.next()
                for f in range(32):
                    p.op('pe', lambda e: e.matmul(y[:], hid[:, f, s * 128:(s + 1) * 128],
                                                  W2[:, f, h * 512:(h + 1) * 512],
                                                  start=(f == 0), stop=(f == 31)),
                         reads=[('f_hid', f), ('W2', f)], writes=[yk])
                p.op('dve', lambda e: e.tensor_tensor(xt[:, s, h * 512:(h + 1) * 512], y[:],
                                                      xt[:, s, h * 512:(h + 1) * 512], ALU.add),
                     reads=[yk, xk], writes=[xk])
        if g + 1 < NG:
            transposes(nxt)
        p.dma('sp', xd[g], xt[:], reads=[xk], writes=[('xdst', g)], out_final=is_final)
    p.scope_end()


def bc(ap, shape, axis):
    return ap.unsqueeze(axis).broadcast_to(shape)


def memkv_phase(p, c, mem, mem_g, w_kv, kg2):
    KmT = [p.sb('KmT%d' % l, [64, 4, 256], BF16) for l in range(2)]
    Vm = p.sb('Vm', [128, 2, 4, 65], BF16)
    c['KmT'], c['Vm'] = KmT, Vm
    p.scope_begin()
    Wkv = p.sb('m_Wkv', [128, 8, 512], BF16)
    load_w_bf16(p, Wkv, 'm_Wkv', w_kv, 8, 512)
    gb = p.sb('m_gb', [128, D], F32)
    p.dma('sp', gb[:], mem_g.partition_broadcast(128), writes=['m_gb'])
    kgb = p.sb('m_kgb', [128, 2, 64], F32)
    for l in range(2):
        p.dma('sp', kgb[:, l, :], kg2[l].partition_broadcast(128), writes=[('m_kgb', l)])
    mt_ = p.sb('m_x', [128, 2, D], F32)
    p.dma('sp', mt_[:], mem.rearrange("(s q) d -> q s d", q=128), writes=['m_x'])
    hT = p.sb('m_hT', [128, 8, 256], BF16)
    nt = NormT(p, c, 'm_nt')
    kvp = p.ps('m_kvp', [128, 512], F32)
    ktp = p.ps('m_ktp', [64, 4, 128], BF16)
    p.excl.update(['m_kvp', 'm_ktp'])
    sq = p.sb('m_sq', [128, 256], F32)
    ssum = p.sb('m_ssum', [128, 4], F32)
    kn = p.sb('m_kn', [128, 256], F32)
    kl = p.sb('m_kl', [128, 256], BF16)
    p.op('pool', lambda e: e.memset(Vm[:, :, :, 64:65], 1.0), writes=['Vm1'])
    for s in range(2):
        nt.emit(mt_[:, s, :], 'm_x', gb[:], 'm_gb', hT, ['m_hT'], s * 128)
    for s in range(2):
        for k in range(8):
            p.op('pe', lambda e: e.matmul(kvp[:], hT[:, k, s * 128:(s + 1) * 128], Wkv[:, k, :],
                                          start=(k == 0), stop=(k == 7)),
                 reads=['m_hT', ('m_Wkv', k)], writes=['m_kvp'])
        p.op('act', lambda e: e.activation(sq[:], kvp[:, 0:256], AF.Square), reads=['m_kvp'], writes=['m_sq'])
        p.op('dve', lambda e: e.tensor_reduce(ssum[:], sq[:].rearrange("p (h d) -> p h d", h=4), AX.X, ALU.add),
             reads=['m_sq'], writes=['m_ssum'])
        p.op('act', lambda e: e.activation(ssum[:], ssum[:], AF.Sqrt, bias=c['eps_rms'][:], scale=1.0 / 64),
             reads=['m_ssum'], writes=['m_ssum'])
        p.op('dve', lambda e: e.reciprocal(ssum[:], ssum[:]), reads=['m_ssum'], writes=['m_ssum'])
        p.op('dve', lambda e: e.tensor_tensor(kn[:].rearrange("p (h d) -> p h d", h=4),
                                              kvp[:, 0:256].rearrange("p (h d) -> p h d", h=4),
                                              bc(ssum[:], [128, 4, 64], 2), ALU.mult),
             reads=['m_kvp', 'm_ssum'], writes=['m_kn'])
        p.op('act', lambda e: e.activation(Vm[:, s, :, 0:64], kvp[:, 256:512].rearrange("p (h d) -> p h d", h=4),
                                           AF.Copy), reads=['m_kvp'], writes=[('Vm', s)])
        for l in range(2):
            p.op('dve', lambda e: e.tensor_tensor(kl[:].rearrange("p (h d) -> p h d", h=4),
                                                  kn[:].rearrange("p (h d) -> p h d", h=4),
                                                  bc(kgb[:, l, :], [128, 4, 64], 1), ALU.mult),
                 reads=['m_kn', ('m_kgb', l)], writes=['m_kl'])
            for h in range(4):
                p.op('pe', lambda e: e.transpose(ktp[:, h, :], kl[:, h * 64:(h + 1) * 64], c['idb'][:]),
                     reads=['m_kl', 'c_idb'], writes=['m_ktp'])
            p.op('act', lambda e: e.activation(KmT[l][:, :, s * 128:(s + 1) * 128], ktp[:], AF.Copy),
                 reads=['m_ktp'], writes=[('KmT', l, s)])
    p.scope_end()


class MemAttn:
    def __init__(self, p, c, name, qg_dram, layer, TP, tpk):
        self.p, self.c, self.l = p, c, layer
        self.n = name
        self.qgb = p.sb(name + '_qgb', [128, 64], F32)
        p.dma('sp', self.qgb[:], qg_dram.partition_broadcast(128), writes=[name + '_qgb'])
        self.sq = p.sb(name + '_sq', [128, 256], F32)
        self.ssum = p.sb(name + '_ssum', [128, 4], F32)
        self.qn = p.sb(name + '_qn', [128, 256], F32)
        self.qb = p.sb(name + '_qb', [128, 256], BF16)
        self.qT = p.sb(name + '_qT', [64, 4, 128], BF16)
        self.pT = p.sb(name + '_pT', [128, 8, 128], BF16)
        self.rc = p.sb(name + '_rc', [128, 4], F32)
        self.TP, self.tpk = TP, tpk

    def emit(self, qps, qkey, sc, sckey, oc, ockey, out_ap, outkey, extra_sc_keys=(), extra_out_keys=()):
        p, c, n, l = self.p, self.c, self.n, self.l
        KmT, Vm = c['KmT'][l], c['Vm']
        v4 = lambda a: a.rearrange("p (h d) -> p h d", h=4)
        p.op('act', lambda e: e.activation(self.sq[:], qps, AF.Square), reads=[qkey], writes=[n + 'sq'])
        p.op('dve', lambda e: e.tensor_reduce(self.ssum[:], v4(self.sq[:]), AX.X, ALU.add),
             reads=[n + 'sq'], writes=[n + 'ss'])
        p.op('act', lambda e: e.activation(self.ssum[:], self.ssum[:], AF.Sqrt, bias=c['eps_rms'][:], scale=1.0 / 64),
             reads=[n + 'ss'], writes=[n + 'ss'])
        p.op('dve', lambda e: e.reciprocal(self.ssum[:], self.ssum[:]), reads=[n + 'ss'], writes=[n + 'ss'])
        p.op('dve', lambda e: e.tensor_tensor(v4(self.qn[:]), v4(qps), bc(self.ssum[:], [128, 4, 64], 2), ALU.mult),
             reads=[qkey, n + 'ss'], writes=[n + 'qn'])
        p.op('dve', lambda e: e.tensor_tensor(v4(self.qb[:]), v4(self.qn[:]), bc(self.qgb[:], [128, 4, 64], 1), ALU.mult),
             reads=[n + 'qn', n + '_qgb'], writes=[n + 'qb'])
        tpv = self.TP[0:64, 0:512].rearrange("p (h t) -> p h t", h=4)
        for h in range(4):
            p.op('pe', lambda e: e.transpose(tpv[:, h, :], self.qb[:, h * 64:(h + 1) * 64], c['idb'][:]),
                 reads=[n + 'qb', 'c_idb'], writes=[self.tpk])
        p.op('act', lambda e: e.activation(self.qT[:], tpv, AF.Copy), reads=[self.tpk], writes=[n + 'qT'])
        for h in range(4):
            for mc in range(2):
                p.op('pe', lambda e: e.matmul(sc[:, h * 2 + mc, :], KmT[:, h, mc * 128:(mc + 1) * 128],
                                              self.qT[:, h, :], start=True, stop=True),
                     reads=[n + 'qT', ('KmT', l, mc)], writes=[sckey], rt=0)
        p.op('act', lambda e: e.activation(self.pT[:], sc, AF.Exp, scale=0.125), reads=[sckey], writes=[n + 'pT'])
        for h in range(4):
            for mc in range(2):
                p.op('pe', lambda e: e.matmul(oc[:, h, :], self.pT[:, h * 2 + mc, :], Vm[:, mc, h, :],
                                              start=(mc == 0), stop=(mc == 1)),
                     reads=[n + 'pT', ('Vm', mc), 'Vm1'], writes=[ockey])
        p.op('dve', lambda e: e.reciprocal(self.rc[:], oc[:, :, 64]), reads=[ockey], writes=[n + 'rc'])
        p.op('dve', lambda e: e.tensor_tensor(v4(out_ap), oc[:, :, 0:64], bc(self.rc[:], [128, 4, 64], 2), ALU.mult),
             reads=[ockey, n + 'rc'], writes=[outkey] + list(extra_out_keys))


C0 = 0.6065306597126334
GN_EPS = 64e-5
NT_ = T // 128
import os as _os
_STOP = _os.environ.get('RW_STOP', '')


def rwkv_phase(p, c, x_src, x_dst, W, ntiles=NT_):
    p.scope_begin()
    Win = p.sb('r_Win', [128, 8, 2816], BF16)
    load_w_bf16(p, Win, 'r_Win', W['rw_in'], 8, 2816)
    Wout = p.sb('r_Wout', [128, 8, D], BF16)
    load_w_bf16(p, Wout, 'r_Wout', W['w_out'], 8, D)
    LW = p.sb('r_LW', [128, 768], BF16)
    p.dma('pool', LW[0:64, :], W['rw_w2'], writes=['r_LWa'])
    p.dma('pool', LW[64:128, :], W['rw_a2'], writes=['r_LWb'])
    G2 = p.sb('r_G2', [128, 768], BF16)
    p.dma('pool', G2[:], W['rw_g2'], writes=['r_G2'])
    colp = p.sb('r_colp', [128, 5, 6], F32)
    p.dma('sp', colp[:], W['colp'], writes=['r_colp'])
    mu = p.sb('r_mu', [128, 20], F32)
    p.dma('sp', mu[:], W['mu'], writes=['r_mu'])
    omk = p.sb('r_omk', [128, 6], F32)
    p.op('dve', lambda e: e.tensor_scalar(omk[:], colp[:, 3, :], -1.0, 1.0, ALU.mult, ALU.add),
         reads=['r_colp'], writes=['r_omk'])
    gb = p.sb('r_gb', [128, D], F32)
    p.dma('sp', gb[:], W['norm_mix_g'].partition_broadcast(128), writes=['r_gb'])
    lng = p.sb('r_lng', [128, 768], F32)
    p.dma('sp', lng[:], W['rw_lnx_g'].partition_broadcast(128), writes=['r_lng'])
    lnb = p.sb('r_lnb', [128, 768], F32)
    p.dma('sp', lnb[:], W['rw_lnx_b'].partition_broadcast(128), writes=['r_lnb'])
    ones = c['ones']
    m_su = p.sb('r_msu', [128, 128], F32)
    p.op('pool', lambda e: e.affine_select(out=m_su[:], in_=ones[:], pattern=[[1, 128]], compare_op=ALU.is_gt,
                                           fill=0.0, base=0, channel_multiplier=-1), reads=['c_ones'], writes=['r_msu'])
    m_u = p.sb('r_mu_', [128, 128], F32)
    p.op('pool', lambda e: e.affine_select(out=m_u[:], in_=ones[:], pattern=[[1, 128]], compare_op=ALU.is_ge,
                                           fill=0.0, base=0, channel_multiplier=-1), reads=['c_ones'], writes=['r_mu_'])
    m_sl = p.sb('r_msl', [128, 128], F32)
    p.op('pool', lambda e: e.affine_select(out=m_sl[:], in_=ones[:], pattern=[[-1, 128]], compare_op=ALU.is_gt,
                                           fill=0.0, base=0, channel_multiplier=1), reads=['c_ones'], writes=['r_msl'])
    hind = p.sb('r_hind', [128, 2], F32)
    p.op('pool', lambda e: e.memset(hind[:], 0.0), writes=['r_hind'])
    p.op('pool', lambda e: e.memset(hind[0:64, 0:1], 1.0), writes=['r_hind'])
    p.op('pool', lambda e: e.memset(hind[64:128, 1:2], 1.0), writes=['r_hind'])
    bones = p.sb('r_bones', [128, 128], F32)
    p.op('pool', lambda e: e.memset(bones[:], 0.0), writes=['r_bones'])
    p.op('pool', lambda e: e.memset(bones[0:64, 0:64], 1.0), writes=['r_bones'])
    p.op('pool', lambda e: e.memset(bones[64:128, 64:128], 1.0), writes=['r_bones'])
    rmask = p.sb('r_rmask', [128, 6, 128], BF16)
    p.op('pool', lambda e: e.memset(rmask[:], 1.0), writes=['r_rmask'])
    p.op('pool', lambda e: e.memset(rmask[:, :, 0:1], 0.0), writes=['r_rmask'])
    gneps = p.sb('r_gneps', [128, 1], F32)
    p.op('pool', lambda e: e.memset(gneps[:], GN_EPS), writes=['r_gneps'])
    TP = p.ps('r_TP', [128, 1024], BF16)
    X = p.ps('r_X', [128, 3, 512], F32)
    WS = p.ps('r_WS', [128, 2, 512], F32)
    PY = p.ps('r_PY', [128, 2, 512], F32)
    p.excl.update(['r_TP', 'r_X0', 'r_X1', 'r_X2', 'r_WS', 'r_PY'])
    ws6 = WS[:].rearrange("p b (j t) -> p (b j) t", t=128)[:, 0:6, :]
    py8 = PY[:].rearrange("p b (j t) -> p (b j) t", t=128)
    tp8 = TP[:].rearrange("p (k t) -> p k t", t=128)
    x12 = X[:, 0:2, :].rearrange("p b (h v) -> p (b h) v", v=64)[:, 0:12, :]
    X2b = X[:, 2, :].bitcast(BF16)
    x2tp8 = X2b.rearrange("p (k t) -> p k t", t=128)
    XK01 = ['r_X0', 'r_X1']

    class _TPRot:
        def next(self_):
            return tp8, 'r_TP'
    nt = NormT(p, c, 'r_nt', tp=_TPRot())
    ma = MemAttn(p, c, 'r_ma', W['mem_q_norm_g'], 0, X2b, 'r_X2')
    hT = p.sb('r_hT', [128, 8, 128], BF16)
    pjs = p.sb('r_pjs', [128, 20, 129], F32)
    p.op('pool', lambda e: e.memset(pjs[:, :, 0:1], 0.0), writes=[('r_pjs', 0), ('r_pjs', 1), ('r_pjs', 2)])
    dtmp = Rot(p, 'r_dtmp', [128, 128], F32, 3)
    slab = p.sb('r_slab', [128, 20, 128], F32)

    def f32t(nm):
        return p.sb('r_' + nm, [128, 6, 128], F32)
    sg, cs, E1, E3, av, kk, k2, tmp1 = [f32t(n) for n in
        ['sg', 'cs', 'E1', 'E3', 'av', 'kk', 'k2', 'tmp1']]
    csm, E4, rkr, kkn, E2 = E1, E3, sg, kk, E1
    lora_b = p.sb('r_lorab', [128, 128], BF16)
    Bh = p.sb('r_Bh', [128, 6, 128], BF16)
    Kh = p.sb('r_Kh', [128, 6, 128], BF16)
    nbias = p.sb('r_nbias', [128, 6], F32)
    IF = []
    for par in range(2):
        d = dict(
            AR=p.sb('r_AR', [128, 6, 2, 128], BF16), Bt=p.sb('r_Bt', [128, 6, 128], BF16), Kt=p.sb('r_Kt', [128, 6, 128], BF16),
            Bht=p.sb('r_Bht', [128, 768], BF16), Kht=p.sb('r_Kht', [128, 768], BF16), V32=p.sb('r_V32', [128, 768], F32),
            Vt=p.sb('r_Vt', [128, 768], BF16), sgd=p.sb('r_sgd', [128, 128], BF16), qm=p.sb('r_qm', [128, 256], F32),
            gC=p.sb('r_gC', [128, 6], F32), bsum=p.sb('r_bsum', [128, 12], F32), xt=p.sb('r_x', [128, D], F32))
        d['k'] = lambda nm, par=par: (nm, par)
        IF.append(d)
    M32 = p.sb('r_M32', [128, 6, 64], F32)
    Mblk = p.sb('r_Mblk', [128, 6, 2, 64], BF16)
    p.op('pool', lambda e: e.memset(M32[:], 0.0), writes=['r_M32'])
    p.op('pool', lambda e: e.memset(Mblk[:], 0.0), writes=['r_Mb'])
    NTs = p.sb('r_NTs', [128, 4, 128], BF16)
    Ss = p.sb('r_Ss', [128, 4, 128], BF16)
    Ls = p.sb('r_Ls', [128, 4, 128], BF16)
    TT = p.sb('r_TT', [128, 12, 128], BF16)
    AKT = p.sb('r_AKT', [128, 12, 128], BF16)
    RBT = p.sb('r_RBT', [128, 12, 128], BF16)
    RKT = p.sb('r_RKT', [128, 12, 128], BF16)
    Wsb = p.sb('r_Wsb', [128, 768], BF16)
    Usb = p.sb('r_Usb', [128, 768], BF16)
    st1 = p.sb('r_st1', [128, 12], F32)
    st2 = p.sb('r_st2', [128, 12], F32)
    st3 = p.sb('r_st3', [128, 12], F32)
    ebuf = p.sb('r_ebuf', [128, 768], F32)
    yn = p.sb('r_yn', [128, 768], F32)
    mixf = RBT[:, 0:8, :].rearrange("p a t -> p (a t)")
    mT = AKT[:, 0:8, :]
    KMIX = [('r_RBT', g) for g in range(3)]
    KMT = [('r_AKT', g) for g in range(3)]
    v12 = lambda a: a.rearrange("p (h v) -> p h v", v=64)
    xs = x_src.rearrange("(n q) d -> n q d", q=128)
    xd = x_dst.rearrange("(n q) d -> n q d", q=128)
    allg = lambda nm: [(nm, g) for g in range(3)]

    def front(it):
        I = IF[it % 2]
        K = I['k']
        AR, Bt, Kt, Bht, Kht, V32, Vt, sgd, qm, gC, bsum, xt = (I[n] for n in
            ['AR', 'Bt', 'Kt', 'Bht', 'Kht', 'V32', 'Vt', 'sgd', 'qm', 'gC', 'bsum', 'xt'])
        xk = K('r_x')
        p.dma('sp', xt[:], xs[it], writes=[xk], reads=[('xsrc', it)])
        nt.emit(xt[:], xk, gb[:], 'r_gb', hT, ['r_hT'], 0)
        yield
        def shift(cc):
            dt_, dk_ = dtmp.next()
            p.op('dve', lambda e: e.tensor_tensor(dt_[:], pjs[:, cc, 0:128], pjs[:, cc, 1:129], ALU.subtract),
                 reads=[('r_pjs', cc // 8)], writes=[dk_])
            p.op('dve', lambda e: e.scalar_tensor_tensor(slab[:, cc, :], dt_[:], mu[:, cc:cc + 1],
                                                         pjs[:, cc, 1:129], ALU.mult, ALU.add),
                 reads=[dk_, ('r_pjs', cc // 8), 'r_mu'], writes=[('r_slab', cc)])
        SL = [('r_slab', cc) for cc in range(20)]
        SLL = [('r_slab', 18), ('r_slab', 19)]
        for r in (2, 0, 1):
            ncc = 8 if r < 2 else 4
            for q in range(ncc):
                cc = r * 8 + q
                for k in range(8):
                    p.op('pe', lambda e: e.matmul(py8[:, q, :], Win[:, k, cc * 128:(cc + 1) * 128], hT[:, k, :],
                                                  start=(k == 0), stop=(k == 7)),
                         reads=[('r_Win', k), 'r_hT'], writes=['r_PY'])
                if q % 2 == 1:
                    yield
            p.op('act', lambda e: e.activation(pjs[:, r * 8:r * 8 + ncc, 1:129], py8[:, 0:ncc, :], AF.Copy),
                 reads=['r_PY'], writes=[('r_pjs', r)])
            yield
            for q in range(ncc):
                shift(r * 8 + q)
                if q % 3 == 2:
                    yield
            if r == 2:
                p.op('act', lambda e: e.activation(lora_b[0:64, :], slab[0:64, 18, :], AF.Tanh), reads=SLL, writes=['r_lorab0'])
                p.op('act', lambda e: e.activation(lora_b[64:128, :], slab[64:128, 18, :], AF.Copy), reads=SLL, writes=['r_lorab1'])
                p.op('act', lambda e: e.activation(sgd[:], slab[:, 19, :], AF.Sigmoid), reads=SLL, writes=[K('r_sgd')])
                for j in range(6):
                    p.op('pe', lambda e: e.matmul(ws6[:, j, :], LW[0:64, j * 128:(j + 1) * 128], lora_b[0:64, :],
                                                  start=True, stop=True), reads=['r_LWa', 'r_lorab0'], writes=['r_WS'], rt=0)
                yield
                for j in range(6):
                    p.op('act', lambda e: e.activation(sg[:, j, :], ws6[:, j, :], AF.Sigmoid, bias=colp[:, 0, j:j + 1]),
                         reads=['r_WS', 'r_colp'], writes=['r_sg'])
                yield
                for j in range(6):
                    p.op('pe', lambda e: e.matmul(ws6[:, j, :], LW[64:128, j * 128:(j + 1) * 128], lora_b[64:128, :],
                                                  start=True, stop=True), reads=['r_LWb', 'r_lorab1', 'r_sg'], writes=['r_WS'], rt=64)
                yield
                for j in range(6):
                    p.op('act', lambda e: e.activation(av[:, j, :], ws6[:, j, :], AF.Sigmoid, bias=colp[:, 1, j:j + 1]),
                         reads=['r_WS', 'r_colp'], writes=['r_av'])
                yield
                p.op('dve', lambda e: e.tensor_tensor_scan(cs[:].rearrange("p j t -> p (j t)"),
                                                           rmask[:].rearrange("p j t -> p (j t)"),
                                                           sg[:].rearrange("p j t -> p (j t)"), 0.0, ALU.mult, ALU.add),
                     reads=['r_sg', 'r_rmask'], writes=['r_cs'])
                p.op('dve', lambda e: e.tensor_tensor(csm[:], cs[:], sg[:], ALU.subtract), reads=['r_cs', 'r_sg'], writes=['r_E1'])
                p.op('act', lambda e: e.activation(E1[:], csm[:], AF.Exp, scale=-C0), reads=['r_E1'], writes=['r_E1'])
                yield
                p.op('act', lambda e: e.activation(E3[:], cs[:], AF.Exp, scale=-C0), reads=['r_cs'], writes=['r_E3'])
                p.op('dve', lambda e: e.tensor_scalar(nbias[:], cs[:, :, 127], -C0, None, ALU.mult), reads=['r_cs'], writes=['r_nbias'])
                p.op('act', lambda e: e.activation(gC[:], nbias[:], AF.Exp), reads=['r_nbias'], writes=[K('r_gC')])
                yield
        for k in range(8):
            p.op('pe', lambda e: e.matmul(PY[:, 0, 0:256], hT[:, k, :], Win[:, k, 2560:2816], start=(k == 0), stop=(k == 7)),
                 reads=['r_hT', ('r_Win', k)], writes=['r_PY'])
        p.op('act', lambda e: e.activation(qm[:], PY[:, 0, 0:256], AF.Copy), reads=['r_PY'], writes=[K('r_qm')])
        p.op('act', lambda e: e.activation(pjs[:, :, 0:1], pjs[:, :, 128:129], AF.Copy),
             reads=[('r_pjs', r_) for r_ in range(3)] + SL, writes=[('r_pjs', r_) for r_ in range(3)])
        R_, K_ = slab[:, 0:6, :], slab[:, 6:12, :]
        yield
        for j in range(6):
            p.op('act', lambda e: e.activation(kk[:, j, :], slab[:, 6 + j, :], AF.Copy, scale=colp[:, 2, j:j + 1]),
                 reads=SL + ['r_colp'], writes=['r_kk'])
        p.op('act', lambda e: e.activation(tmp1[:], kk[:], AF.Square), reads=['r_kk'], writes=['r_tmp1'])
        yield
        for j in range(6):
            p.op('pe', lambda e: e.matmul(ws6[:, j, :], bones[:], tmp1[:, j, :], start=True, stop=True),
                 reads=['r_bones', 'r_tmp1', 'r_av'], writes=['r_WS'])
        yield
        p.op('act', lambda e: e.activation(tmp1[:], ws6, AF.Sqrt), reads=['r_WS'], writes=['r_tmp1'])
        p.op('dve', lambda e: e.tensor_scalar(tmp1[:], tmp1[:], 1e-12, None, ALU.max), reads=['r_tmp1'], writes=['r_tmp1'])
        p.op('dve', lambda e: e.reciprocal(tmp1[:], tmp1[:]), reads=['r_tmp1'], writes=['r_tmp1'])
        p.op('dve', lambda e: e.tensor_tensor(kkn[:], tmp1[:], kk[:], ALU.mult), reads=['r_tmp1', 'r_kk'], writes=['r_kk'])
        yield
        for j in range(6):
            p.op('act', lambda e: e.activation(tmp1[:, j, :], av[:, j, :], AF.Identity, scale=colp[:, 3, j:j + 1], bias=omk[:, j:j + 1]),
                 reads=['r_av', 'r_colp', 'r_omk', 'r_WS'], writes=['r_tmp1'])
        yield
        p.op('dve', lambda e: e.tensor_tensor(k2[:], tmp1[:], K_, ALU.mult), reads=['r_tmp1'] + SL, writes=['r_k2'])
        p.op('dve', lambda e: e.tensor_tensor(rkr[:], k2[:], R_, ALU.mult), reads=['r_k2'] + SL, writes=['r_sg'])
        for j in range(6):
            p.op('act', lambda e: e.activation(rkr[:, j, :], rkr[:, j, :], AF.Copy, scale=colp[:, 4, j:j + 1]),
                 reads=['r_sg', 'r_colp'], writes=['r_sg'])
        yield
        p.op('dve', lambda e: e.scalar_tensor_tensor(AR[:, :, 0, :], kkn[:], -1.0, E1[:], ALU.mult, ALU.mult),
             reads=['r_kk', 'r_E1'], writes=[K('r_AR0')])
        p.op('act', lambda e: e.activation(E2[:], cs[:], AF.Exp, scale=C0), reads=['r_cs'], writes=['r_E1'])
        p.op('dve', lambda e: e.tensor_tensor(AR[:, :, 1, :], R_, E3[:], ALU.mult), reads=SL + ['r_E3'], writes=[K('r_AR1')])
        yield
        for j in range(6):
            p.op('act', lambda e: e.activation(E4[:, j, :], cs[:, j, :], AF.Exp, scale=C0, bias=nbias[:, j:j + 1]),
                 reads=['r_cs', 'r_nbias'], writes=['r_E3'])
        p.op('dve', lambda e: e.tensor_tensor(tmp1[:], kkn[:], av[:], ALU.mult), reads=['r_kk', 'r_av', 'r_k2'], writes=['r_tmp1'])
        yield
        p.op('dve', lambda e: e.tensor_tensor(Bt[:], tmp1[:], E2[:], ALU.mult), reads=['r_tmp1', 'r_E1'], writes=[K('r_Bt')])
        p.op('pool', lambda e: e.tensor_tensor(Bh[:], tmp1[:], E4[:], ALU.mult), reads=['r_tmp1', 'r_E3'], writes=['r_Bh'])
        p.op('dve', lambda e: e.tensor_tensor(Kt[:], k2[:], E2[:], ALU.mult), reads=['r_k2', 'r_E1'], writes=[K('r_Kt')])
        p.op('pool', lambda e: e.tensor_tensor(Kh[:], k2[:], E4[:], ALU.mult), reads=['r_k2', 'r_E3'], writes=['r_Kh'])
        yield
        for nm, srcT, dstT in (('r_Bh', Bh, Bht), ('r_Kh', Kh, Kht)):
            for j in range(6):
                p.op('pe', lambda e: e.transpose(tp8[:, j, :], srcT[:, j, :], c['idb'][:]),
                     reads=[nm, 'c_idb'], writes=['r_TP'])
            p.op('act', lambda e: e.activation(dstT[:].rearrange("p (j t) -> p j t", t=128), tp8[:, 0:6, :], AF.Copy),
                 reads=['r_TP'], writes=[K(nm + 't')])
            yield
        for j in range(6):
            p.op('pe', lambda e: e.transpose(ws6[:, j, :], slab[:, 12 + j, :], c['idf'][:]),
                 reads=SL + ['c_idf', 'r_kk'], writes=['r_WS'])
        p.op('act', lambda e: e.activation(V32[:].rearrange("p (j t) -> p j t", t=128), ws6, AF.Copy),
             reads=['r_WS'], writes=[K('r_V32')])
        p.op('dve', lambda e: e.tensor_copy(Vt[:], V32[:]), reads=[K('r_V32')], writes=[K('r_Vt')])
        yield
        for j in range(6):
            p.op('pe', lambda e: e.matmul(WS[:, 0, j * 2:(j + 1) * 2], rkr[:, j, :], hind[:], start=True, stop=True),
                 reads=['r_sg', 'r_hind'], writes=['r_WS'])
        p.op('act', lambda e: e.activation(bsum[:], WS[:, 0, 0:12], AF.Copy), reads=['r_WS'], writes=[K('r_bsum')])
        yield

    def back(it):
        I = IF[it % 2]
        K = I['k']
        AR, Bt, Kt, Bht, Kht, V32, Vt, sgd, qm, gC, bsum, xt = (I[n] for n in
            ['AR', 'Bt', 'Kt', 'Bht', 'Kht', 'V32', 'Vt', 'sgd', 'qm', 'gC', 'bsum', 'xt'])
        xk = K('r_x')
        kAR0, kAR1, kBt, kKt = K('r_AR0'), K('r_AR1'), K('r_Bt'), K('r_Kt')
        x4 = lambda b: X[:, b, :].rearrange("p (i t) -> p i t", t=128)
        for g in range(3):
            hs = [g * 4 + i for i in range(4)]
            for i, h in enumerate(hs):
                j, P0 = h // 2, (h % 2) * 64
                sl = slice(P0, P0 + 64)
                p.op('pe', lambda e: e.matmul(X[:, 0, i * 128:(i + 1) * 128], Bt[sl, j, :], AR[sl, j, 0, :], start=True, stop=True),
                     reads=[kBt, kAR0], writes=['r_X0'], rt=P0)
                p.op('pe', lambda e: e.matmul(X[:, 1, i * 128:(i + 1) * 128], AR[sl, j, 0, :], Bt[sl, j, :], start=True, stop=True),
                     reads=[kBt, kAR0], writes=['r_X1'], rt=P0)
                p.op('pe', lambda e: e.matmul(X[:, 2, i * 128:(i + 1) * 128], Kt[sl, j, :], AR[sl, j, 0, :], start=True, stop=True),
                     reads=[kKt, kAR0], writes=['r_X2'], rt=P0)
            p.op('dve', lambda e: e.tensor_tensor(NTs[:], x4(0), bc(m_su[:], [128, 4, 128], 1), ALU.mult),
                 reads=['r_X0', 'r_msu'], writes=['r_NTs'])
            p.op('dve', lambda e: e.tensor_tensor(Ls[:], x4(1), bc(m_sl[:], [128, 4, 128], 1), ALU.mult),
                 reads=['r_X1', 'r_msl'], writes=['r_Ls'])
            p.op('dve', lambda e: e.tensor_tensor(AKT[:, g * 4:(g + 1) * 4, :], x4(2), bc(m_su[:], [128, 4, 128], 1), ALU.mult),
                 reads=['r_X2', 'r_msu'], writes=[('r_AKT', g)])
            yield
            for i, h in enumerate(hs):
                j, P0 = h // 2, (h % 2) * 64
                sl = slice(P0, P0 + 64)
                p.op('pe', lambda e: e.matmul(X[:, 0, i * 128:(i + 1) * 128], Bt[sl, j, :], AR[sl, j, 1, :], start=True, stop=True),
                     reads=[kBt, kAR1, 'r_NTs'], writes=['r_X0'], rt=P0)
                p.op('pe', lambda e: e.matmul(X[:, 2, i * 128:(i + 1) * 128], Kt[sl, j, :], AR[sl, j, 1, :], start=True, stop=True),
                     reads=[kKt, kAR1, ('r_AKT', g)], writes=['r_X2'], rt=P0)
            p.op('dve', lambda e: e.tensor_tensor(RBT[:, g * 4:(g + 1) * 4, :], x4(0), bc(m_u[:], [128, 4, 128], 1), ALU.mult),
                 reads=['r_X0', 'r_mu_'], writes=[('r_RBT', g)])
            p.op('dve', lambda e: e.tensor_tensor(RKT[:, g * 4:(g + 1) * 4, :], x4(2), bc(m_u[:], [128, 4, 128], 1), ALU.mult),
                 reads=['r_X2', 'r_mu_'], writes=[('r_RKT', g)])
            p.op('pool', lambda e: e.tensor_tensor(Ss[:], NTs[:], bc(c['idb'][:], [128, 4, 128], 1), ALU.add),
                 reads=['r_NTs', 'c_idb'], writes=['r_Ss'])
            yield
            for step in range(7):
                last = (step == 6)
                for i in range(4):
                    cs_ = slice(i * 128, (i + 1) * 128)
                    if not last:
                        p.op('pe', lambda e: e.matmul(X[:, 0, cs_], Ls[:, i, :], NTs[:, i, :], start=True, stop=True),
                             reads=['r_Ls', 'r_NTs', ('r_RBT', g)], writes=['r_X0'])
                        p.op('pe', lambda e: e.matmul(X[:, 2, cs_], NTs[:, i, :], Ls[:, i, :], start=True, stop=True),
                             reads=['r_Ls', 'r_NTs', ('r_RKT', g)], writes=['r_X2'])
                    if step > 0:
                        p.op('pe', lambda e: e.matmul(X[:, 1, cs_], Ls[:, i, :], Ss[:, i, :], start=True, stop=True),
                             reads=['r_Ls', 'r_Ss'], writes=['r_X1'])
                if step > 0:
                    dst = TT[:, g * 4:(g + 1) * 4, :] if last else Ss[:]
                    dk = ('r_TT', g) if last else 'r_Ss'
                    p.op('dve', lambda e: e.tensor_tensor(dst, x4(1), Ss[:], ALU.add), reads=['r_X1', 'r_Ss'], writes=[dk])
                if not last:
                    p.op('act', lambda e: e.activation(NTs[:], x4(0), AF.Copy), reads=['r_X0'], writes=['r_NTs'])
                    p.op('act', lambda e: e.activation(Ls[:], x4(2), AF.Copy), reads=['r_X2'], writes=['r_Ls'])
                yield
        for j in range(6):
            p.op('pe', lambda e: e.matmul(x12[:, 2 * j:2 * j + 2, :], AR[:, j, 0, :], Mblk[:, j, :, :].rearrange("p a v -> p (a v)"),
                                          start=True, stop=False),
                 reads=[kAR0, 'r_Mb'], writes=XK01)
            for h in (2 * j, 2 * j + 1):
                p.op('pe', lambda e: e.matmul(x12[:, h, :], AKT[:, h, :], Vt[:, h * 64:(h + 1) * 64], start=False, stop=(h % 2 == 1)),
                     reads=allg('r_AKT') + [K('r_Vt')], writes=XK01)
        p.op('act', lambda e: e.activation(v12(Wsb[:]), x12, AF.Copy), reads=XK01, writes=['r_Wsb'])
        yield
        for h in range(12):
            p.op('pe', lambda e: e.matmul(x12[:, h, :], TT[:, h, :], Wsb[:, h * 64:(h + 1) * 64], start=True, stop=True),
                 reads=allg('r_TT') + ['r_Wsb'], writes=XK01)
        p.op('act', lambda e: e.activation(v12(Usb[:]), x12, AF.Copy), reads=XK01, writes=['r_Usb'])
        yield
        MP = X[:, 2, 0:384].rearrange("p (j v) -> p j v", v=64)
        for h in range(12):
            j, P0 = h // 2, (h % 2) * 64
            sl = slice(P0, P0 + 64)
            hv = slice(h * 64, (h + 1) * 64)
            p.op('pe', lambda e: e.matmul(MP[sl, j, :], Bht[:, hv], Usb[:, hv], start=True, stop=False),
                 reads=[K('r_Bht'), 'r_Usb'], writes=['r_X2'])
            p.op('pe', lambda e: e.matmul(MP[sl, j, :], Kht[:, hv], Vt[:, hv], start=False, stop=True),
                 reads=[K('r_Kht'), K('r_Vt')], writes=['r_X2'])
        for j in range(6):
            p.op('pe', lambda e: e.matmul(x12[:, 2 * j:2 * j + 2, :], AR[:, j, 1, :], Mblk[:, j, :, :].rearrange("p a v -> p (a v)"),
                                          start=True, stop=False),
                 reads=[kAR1, 'r_Mb'], writes=XK01)
            for h in (2 * j, 2 * j + 1):
                hv = slice(h * 64, (h + 1) * 64)
                p.op('pe', lambda e: e.matmul(x12[:, h, :], RBT[:, h, :], Usb[:, hv], start=False, stop=False),
                     reads=allg('r_RBT') + ['r_Usb'], writes=XK01)
                p.op('pe', lambda e: e.matmul(x12[:, h, :], RKT[:, h, :], Vt[:, hv], start=False, stop=(h % 2 == 1)),
                     reads=allg('r_RKT') + [K('r_Vt')], writes=XK01)
        p.op('dve', lambda e: e.tensor_tensor(M32[:], M32[:], bc(gC[:], [128, 6, 64], 2), ALU.mult),
             reads=['r_M32', K('r_gC')], writes=['r_M32'])
        p.op('dve', lambda e: e.tensor_tensor(M32[:], M32[:], MP, ALU.add), reads=['r_M32', 'r_X2'], writes=['r_M32'])
        p.op('act', lambda e: e.activation(Mblk[0:64, :, 0, :], M32[0:64, :, :], AF.Copy), reads=['r_M32'], writes=['r_Mb'])
        p.op('act', lambda e: e.activation(Mblk[64:128, :, 1, :], M32[64:128, :, :], AF.Copy), reads=['r_M32'], writes=['r_Mb'])
        yield
        p.op('act', lambda e: e.activation(v12(ebuf[:]), x12, AF.Square), reads=XK01, writes=['r_ebuf'])
        p.op('dve', lambda e: e.tensor_reduce(st1[:], x12, AX.X, ALU.add), reads=XK01, writes=['r_st1'])
        p.op('act', lambda e: e.activation(v12(yn[:]), x12, AF.Copy), reads=XK01, writes=['r_yn'])
        p.op('dve', lambda e: e.tensor_reduce(st2[:], v12(ebuf[:]), AX.X, ALU.add), reads=['r_ebuf'], writes=['r_st2'])
        p.op('dve', lambda e: e.tensor_scalar(st1[:], st1[:], 1.0 / 64, None, ALU.mult), reads=['r_st1'], writes=['r_st1'])
        p.op('dve', lambda e: e.tensor_tensor(st3[:], st1[:], st1[:], ALU.mult), reads=['r_st1'], writes=['r_st3'])
        p.op('dve', lambda e: e.scalar_tensor_tensor(st2[:], st2[:], 1.0 / 64, st3[:], ALU.mult, ALU.subtract),
             reads=['r_st2', 'r_st3'], writes=['r_st2'])
        p.op('act', lambda e: e.activation(st2[:], st2[:], AF.Sqrt, bias=gneps[:]), reads=['r_st2', 'r_gneps'], writes=['r_st2'])
        p.op('dve', lambda e: e.reciprocal(st2[:], st2[:]), reads=['r_st2'], writes=['r_st2'])
        yield
        p.op('dve', lambda e: e.tensor_tensor(v12(yn[:]), v12(yn[:]), bc(st1[:], [128, 12, 64], 2), ALU.subtract),
             reads=['r_yn', 'r_st1'], writes=['r_yn'])
        p.op('dve', lambda e: e.tensor_tensor(v12(yn[:]), v12(yn[:]), bc(st2[:], [128, 12, 64], 2), ALU.mult),
             reads=['r_yn', 'r_st2'], writes=['r_yn'])
        p.op('dve', lambda e: e.tensor_tensor(yn[:], yn[:], lng[:], ALU.mult), reads=['r_yn', 'r_lng'], writes=['r_yn'])
        p.op('dve', lambda e: e.tensor_tensor(yn[:], yn[:], lnb[:], ALU.add), reads=['r_yn', 'r_lnb'], writes=['r_yn'])
        p.op('pool', lambda e: e.tensor_tensor(v12(ebuf[:]), v12(V32[:]), bc(bsum[:], [128, 12, 64], 2), ALU.mult),
             reads=[K('r_V32'), K('r_bsum'), 'r_ebuf'], writes=['r_ebuf'])
        p.op('dve', lambda e: e.tensor_tensor(yn[:], yn[:], ebuf[:], ALU.add), reads=['r_yn', 'r_ebuf'], writes=['r_yn'])
        yield
        for i in range(2):
            p.op('pe', lambda e: e.matmul(X[:, i, 0:384], sgd[:], G2[:, i * 384:(i + 1) * 384], start=True, stop=True),
                 reads=[K('r_sgd'), 'r_G2'], writes=['r_X%d' % i])
        for i in range(2):
            p.op('dve', lambda e: e.tensor_tensor(mixf[:, i * 384:(i + 1) * 384], X[:, i, 0:384],
                                                  yn[:, i * 384:(i + 1) * 384], ALU.mult),
                 reads=['r_X%d' % i, 'r_yn'], writes=KMIX)
        yield
        sc = X[:, 0:2, :].rearrange("p b (j t) -> p (b j) t", t=128)
        oc = X[:, 2, 0:260].rearrange("p (h e) -> p h e", e=65)
        ma.emit(qm[:], K('r_qm'), sc, 'r_X0', oc, 'r_X2', mixf[:, 768:1024], KMIX[0], extra_sc_keys=['r_X1'], extra_out_keys=KMIX[1:])
        yield
        for k in range(8):
            p.op('pe', lambda e: e.transpose(x2tp8[:, k, :], mixf[:, k * 128:(k + 1) * 128], c['idb'][:]),
                 reads=KMIX + ['c_idb'], writes=['r_X2'])
        p.op('act', lambda e: e.activation(mT, x2tp8, AF.Copy), reads=['r_X2'], writes=KMT)
        for hh in range(2):
            for k in range(8):
                p.op('pe', lambda e: e.matmul(X[:, hh, :], mT[:, k, :], Wout[:, k, hh * 512:(hh + 1) * 512],
                                              start=(k == 0), stop=(k == 7)),
                     reads=KMT + [('r_Wout', k)], writes=['r_X%d' % hh])
            p.op('dve', lambda e: e.tensor_tensor(xt[:, hh * 512:(hh + 1) * 512], X[:, hh, :],
                                                  xt[:, hh * 512:(hh + 1) * 512], ALU.add),
                 reads=['r_X%d' % hh, xk], writes=[xk])
        p.dma('sp', xd[it], xt[:], reads=[xk], writes=[('xdst', it)])
        yield

    for _ in front(0):
        pass
    for it in range(ntiles):
        gens = [back(it)]
        if it + 1 < ntiles:
            gens.append(front(it + 1))
        nb_ = int(_os.environ.get('RW_RATIO', '1'))
        while gens:
            for g_ in list(gens):
                try:
                    for _r in range(nb_ if g_ is gens[0] and len(gens) > 1 else 1):
                        next(g_)
                except StopIteration:
                    gens.remove(g_)
    p.scope_end()


import math as _math
LAMBDA_INIT = 0.8 - 0.6 * _math.exp(-0.3 * 1)
SLOPES = [2.0 ** (-8.0 * (h + 1) / 6) for h in range(6)]


def diff_phase(p, c, x_src, x_dst, mo_dram, qs_dram, W, ntiles=NT_):
    NTL = ntiles
    TL = NTL * 128
    p.scope_begin()
    qrot = Rot(p, 'd_qt', [128, 6, 128], BF16, 2)
    KT = p.sb('d_KT', [128, 6, TL], BF16)
    VA = p.sb('d_VA', [128, NTL, 6, 129], BF16)
    p.op('pool', lambda e: e.memset(VA[:, :, :, 128:129], 1.0), writes=['d_VA1'])
    TPf = p.ps('d_TP', [128, 512], F32)
    TP = TPf[:].bitcast(BF16)
    PB = p.ps('d_PB', [128, 7, 512], F32)
    p.excl.update(['d_TP'] + [('d_PB', i) for i in range(7)])
    tp8 = TP.rearrange("p (k t) -> p k t", t=128)

    class _TPRot:
        def next(self_):
            return tp8, 'd_TP'
    lq = p.sb('d_lq', [128, 4, 64], F32)
    for i, nm in enumerate(['df_lq1', 'df_lk1', 'df_lq2', 'df_lk2']):
        p.dma('sp', lq[:, i, :], W[nm].partition_broadcast(128), writes=[('d_lq', i)])
    lpr = p.sb('d_lpr', [128, 2, 64], F32)
    lsm = p.sb('d_lsm', [128, 2], F32)
    nlam = p.sb('d_nlam', [128, 1], F32)
    p.op('dve', lambda e: e.tensor_tensor(lpr[:, 0, :], lq[:, 0, :], lq[:, 1, :], ALU.mult), reads=[('d_lq', 0), ('d_lq', 1)], writes=['d_lpr0'])
    p.op('dve', lambda e: e.tensor_tensor(lpr[:, 1, :], lq[:, 2, :], lq[:, 3, :], ALU.mult), reads=[('d_lq', 2), ('d_lq', 3)], writes=['d_lpr1'])
    p.op('dve', lambda e: e.tensor_reduce(lsm[:], lpr[:], AX.X, ALU.add), reads=['d_lpr0', 'd_lpr1'], writes=['d_lsm'])
    p.op('act', lambda e: e.activation(lsm[:], lsm[:], AF.Exp), reads=['d_lsm'], writes=['d_lsm'])
    p.op('dve', lambda e: e.tensor_tensor(nlam[:], lsm[:, 1:2], lsm[:, 0:1], ALU.subtract), reads=['d_lsm'], writes=['d_nlam'])
    p.op('dve', lambda e: e.tensor_scalar(nlam[:], nlam[:], -LAMBDA_INIT, None, ALU.add), reads=['d_nlam'], writes=['d_nlam'])
    tabi = p.sb('d_tabi', [128, 32], F32)
    p.op('pool', lambda e: e.iota(tabi[:], pattern=[[128, 32]], base=-127 - 128 * 31, channel_multiplier=1,
                                  allow_small_or_imprecise_dtypes=True), writes=['d_tabi'])
    tab = p.sb('d_tab', [128, 6, 32], F32)
    for h in range(6):
        p.op('pool', lambda e: e.tensor_scalar(tab[:, h, :], tabi[:], SLOPES[h], None, ALU.mult), reads=['d_tabi'], writes=['d_tab'])
    m_u = p.sb('d_mu', [128, 128], BF16)
    p.op('pool', lambda e: e.affine_select(out=m_u[:], in_=c['ones'][:], pattern=[[1, 128]], compare_op=ALU.is_ge,
                                           fill=0.0, base=0, channel_multiplier=-1), reads=['c_ones'], writes=['d_mu'])
    p.scope_begin()
    Win = p.sb('d_Win', [128, 8, 2560], BF16)
    load_w_bf16(p, Win, 'd_Win', W['df_in'], 8, 2560)
    gb = p.sb('d_gb', [128, D], F32)
    p.dma('sp', gb[:], W['norm_mix_g'].partition_broadcast(128), writes=['d_gb'])
    qkg = p.sb('d_qkg', [128, 2, 128], F32)
    p.dma('sp', qkg[:, 0, :], W['df_q_norm_g'].partition_broadcast(128), writes=[('d_qkg', 0)])
    p.dma('sp', qkg[:, 1, :], W['df_k_norm_g'].partition_broadcast(128), writes=[('d_qkg', 1)])
    nt = NormT(p, c, 'd_nt', tp=_TPRot())
    ma = MemAttn(p, c, 'd_ma', W['mem_q_norm_g'], 1, TP, 'd_TP')
    xg = Rot(p, 'd_x', [128, D], F32, 1)
    hT = p.sb('d_hT', [128, 8, 128], BF16)
    sq = p.sb('d_sq', [128, 768], F32)
    ssm = p.sb('d_ssm', [128, 12], F32)
    qn = sq
    qb = p.sb('d_qb', [128, 768], BF16)
    mo = Rot(p, 'd_mo', [128, 256], BF16, 2)
    xs = x_src.rearrange("(n q) d -> n q d", q=128)
    xd = x_dst.rearrange("(n q) d -> n q d", q=128)
    mod = mo_dram.rearrange("(n q) d -> n q d", q=128)
    v6 = lambda a: a.rearrange("p (h e) -> p h e", h=6)
    v12 = lambda a: a.rearrange("p (g d) -> p g d", d=64)
    qmr = Rot(p, 'd_qm', [128, 256], F32, 2)

    def proj(b):
        c0 = b * 384 if b < 6 else 2304
        n = 384 if b < 6 else 256
        for k in range(8):
            p.op('pe', lambda e: e.matmul(PB[:, b, 0:n], hT[:, k, :], Win[:, k, c0:c0 + n], start=(k == 0), stop=(k == 7)),
                 reads=['d_hT', ('d_Win', k)], writes=[('d_PB', b)])

    def normproj_qk(it):
        xt, xk = xg.next()
        p.dma('sp', xt[:], xs[it], writes=[xk], reads=[('xsrc', it)])
        nt.emit(xt[:], xk, gb[:], 'd_gb', hT, ['d_hT'], 0)
        for b in range(4):
            proj(b)

    def proj_vqm(it):
        for b in range(4, 7):
            proj(b)

    def post(it):
        qtl, qtk = qrot.next()
        for qi, (dst, b0) in enumerate(((None, 0), (KT, 2))):
            for i in range(2):
                p.op('act', lambda e: e.activation(sq[:, i * 384:(i + 1) * 384], PB[:, b0 + i, 0:384], AF.Square),
                     reads=[('d_PB', b0 + i)], writes=['d_sq'])
            p.op('dve', lambda e: e.tensor_reduce(ssm[:], v12(sq[:]), AX.X, ALU.add), reads=['d_sq'], writes=['d_ssm'])
            p.op('act', lambda e: e.activation(ssm[:], ssm[:], AF.Sqrt, bias=c['eps_rms'][:], scale=1.0 / 64), reads=['d_ssm'], writes=['d_ssm'])
            p.op('dve', lambda e: e.reciprocal(ssm[:], ssm[:]), reads=['d_ssm'], writes=['d_ssm'])
            for i in range(2):
                p.op('dve', lambda e: e.tensor_tensor(v12(qn[:, i * 384:(i + 1) * 384]), v12(PB[:, b0 + i, 0:384]),
                                                      bc(ssm[:, i * 6:(i + 1) * 6], [128, 6, 64], 2), ALU.mult),
                     reads=[('d_PB', b0 + i), 'd_ssm'], writes=['d_sq'])
            p.op('dve', lambda e: e.tensor_tensor(v6(qb[:]), v6(qn[:]), bc(qkg[:, qi, :], [128, 6, 128], 1), ALU.mult),
                 reads=['d_sq', ('d_qkg', qi)], writes=['d_qb'])
            for h in range(6):
                p.op('pe', lambda e: e.transpose(tp8[:, h, :], qb[:, h * 128:(h + 1) * 128], c['idb'][:]),
                     reads=['d_qb', 'c_idb'], writes=['d_TP'])
            if qi == 0:
                p.op('act', lambda e: e.activation(qtl[:], tp8[:, 0:6, :], AF.Copy), reads=['d_TP'], writes=[qtk])
                p.dma('sp', qs_dram[it], qtl[:].rearrange("p h t -> p (h t)"), reads=[qtk], writes=[('d_QK', 0, it)])
            else:
                p.op('act', lambda e: e.activation(dst[:, :, it * 128:(it + 1) * 128], tp8[:, 0:6, :], AF.Copy),
                     reads=['d_TP'], writes=[('d_QK', qi, it)])
        for i in range(2):
            p.op('act', lambda e: e.activation(VA[:, it, i * 3:(i + 1) * 3, 0:128],
                                               PB[:, 4 + i, 0:384].rearrange("p (h e) -> p h e", h=3), AF.Copy),
                 reads=[('d_PB', 4 + i)], writes=[('d_VA', it, i)])
        qmt, qmk = qmr.next()
        p.op('act', lambda e: e.activation(qmt[:], PB[:, 6, 0:256], AF.Copy), reads=[('d_PB', 6)], writes=[qmk])
        return qmt, qmk

    def memattn(it, qmt, qmk):
        mt, mk = mo.next()
        sc = PB[:, 4:6, :].rearrange("p b (j t) -> p (b j) t", t=128)
        oc = PB[:, 6, 0:260].rearrange("p (h e) -> p h e", e=65)
        ma.emit(qmt[:], qmk, sc, ('d_PB', 4), oc, ('d_PB', 6), mt[:], mk, extra_sc_keys=[('d_PB', 5)])
        p.dma('sp', mod[it], mt[:], reads=[mk], writes=[('mo', it)])

    normproj_qk(0)
    proj_vqm(0)
    for it in range(NTL):
        qmt, qmk = post(it)
        if it + 1 < NTL:
            normproj_qk(it + 1)
        memattn(it, qmt, qmk)
        if it + 1 < NTL:
            proj_vqm(it + 1)
    p.scope_end()
    p.scope_begin()
    Wout = p.sb('d_Wout', [128, 8, D], BF16)
    load_w_bf16(p, Wout, 'd_Wout', W['w_out'], 8, D)
    sgb = p.sb('d_sgb', [128, 128], F32)
    p.dma('sp', sgb[:], W['df_subln_g'].partition_broadcast(128), writes=['d_sgb'])
    p.op('dve', lambda e: e.tensor_scalar(sgb[:], sgb[:], 1.0 - LAMBDA_INIT, None, ALU.mult), reads=['d_sgb'], writes=['d_sgb'])
    xg = Rot(p, 'd_x2', [128, D], F32, 2)
    qblk = Rot(p, 'd_qblk', [128, 6, 2, 128], BF16, 2)
    for i in range(2):
        p.op('pool', lambda e: e.memset(qblk.tiles[i][:], 0.0), writes=[qblk.keys[i]])
    pT = Rot(p, 'd_pT', [128, 2, 128], BF16, 6)
    mixf = Rot(p, 'd_mixf', [128, D], BF16, 2)
    mT = p.sb('d_mT', [128, 8, 128], BF16)
    o0 = Rot(p, 'd_o0', [128, 768], F32, 2)
    o1 = Rot(p, 'd_o1', [128, 128], F32, 2)
    rc = Rot(p, 'd_rc', [128, 2], F32, 2)
    osq = p.sb('d_osq', [128, 768], F32)
    oss = p.sb('d_oss', [128, 6], F32)
    SB = [0, 1, 6]
    items = [(qi, h, kj) for qi in range(NTL) for h in range(6) for kj in range(qi + 1)]
    state = {}

    def tile_begin(qi):
        xt, xk = xg.next()
        p.dma('sp', xt[:], xs[qi], writes=[xk], reads=[('xsrc', qi)])
        mf, mfk = mixf.next()
        p.dma('sp', mf[:, 768:1024], mod[qi], reads=[('mo', qi)], writes=[(mfk, 1)])
        qb_, qbk = qblk.next()
        qsrc = qs_dram[qi].rearrange("p (h t) -> p h t", h=6)
        p.dma('sp', qb_[0:64, :, 0, :], qsrc[0:64], reads=[('d_QK', 0, qi)], writes=[qbk])
        p.dma('sp', qb_[64:128, :, 1, :], qsrc[64:128], reads=[('d_QK', 0, qi)], writes=[qbk])
        o0t, o0k = o0.next()
        state[qi] = dict(xt=xt, xk=xk, mf=mf, mfk=mfk, qb=qb_, qbk=qbk, o0=o0t, o0k=o0k)

    def emit_scores(n):
        qi, h, kj = items[n]
        if h == 0 and kj == 0:
            tile_begin(qi)
        st = state[qi]
        sb_ = SB[n % 3]
        p.op('pe', lambda e: e.matmul(PB[:, sb_, 0:256], KT[:, h, kj * 128:(kj + 1) * 128],
                                      st['qb'][:, h, :, :].rearrange("p c t -> p (c t)"), start=True, stop=True),
             reads=[('d_QK', 1, kj), st['qbk']], writes=[('d_PB', sb_)])

    def emit_rest(n):
        qi, h, kj = items[n]
        st = state[qi]
        sb_ = SB[n % 3]
        ob = 2 if h % 2 == 0 else 4
        pt, pk = pT.next()
        p.op('act', lambda e: e.activation(pt[:].rearrange("p c t -> p (c t)"), PB[:, sb_, 0:256], AF.Exp, scale=0.125,
                                           bias=tab[:, h, 31 + kj - qi:32 + kj - qi]),
             reads=[('d_PB', sb_), 'd_tab'], writes=[pk])
        if kj == qi:
            p.op('pool', lambda e: e.tensor_tensor(pt[:], pt[:], bc(m_u[:], [128, 2, 128], 1), ALU.mult),
                 reads=[pk, 'd_mu'], writes=[pk])
        for cm in range(2):
            p.op('pe', lambda e: e.matmul(PB[:, ob + cm, 0:129], pt[:, cm, :], VA[:, kj, h, :], start=(kj == 0), stop=(kj == qi)),
                 reads=[pk, ('d_VA', kj, 0), ('d_VA', kj, 1), 'd_VA1'], writes=[('d_PB', ob + cm)])
        if kj == qi:
            head_end(qi, h, ob)
            if h == 5:
                tile_end(qi)

    def head_end(qi, h, ob):
        st = state[qi]
        o0t, o0k = st['o0'], st['o0k']
        r_, rk = rc.next()
        o1t, o1k = o1.next()
        p.op('dve', lambda e: e.reciprocal(r_[:, 0:1], PB[:, ob, 128:129]), reads=[('d_PB', ob)], writes=[rk])
        p.op('dve', lambda e: e.reciprocal(r_[:, 1:2], PB[:, ob + 1, 128:129]), reads=[('d_PB', ob + 1)], writes=[rk])
        p.op('dve', lambda e: e.tensor_scalar(r_[:, 1:2], r_[:, 1:2], nlam[:], None, ALU.mult), reads=[rk, 'd_nlam'], writes=[rk])
        p.op('dve', lambda e: e.tensor_scalar(o1t[:], PB[:, ob + 1, 0:128], r_[:, 1:2], None, ALU.mult),
             reads=[('d_PB', ob + 1), rk], writes=[o1k])
        p.op('dve', lambda e: e.scalar_tensor_tensor(o0t[:, h * 128:(h + 1) * 128], PB[:, ob, 0:128], r_[:, 0:1], o1t[:], ALU.mult, ALU.add),
             reads=[('d_PB', ob), rk, o1k], writes=[(o0k, h)])

    def tile_end(qi):
        st = state.pop(qi)
        xt, xk, mf, mfk, o0t, o0k = st['xt'], st['xk'], st['mf'], st['mfk'], st['o0'], st['o0k']
        oks = [(o0k, h) for h in range(6)]
        p.op('act', lambda e: e.activation(osq[:], o0t[:], AF.Square), reads=oks, writes=['d_osq'])
        p.op('dve', lambda e: e.tensor_reduce(oss[:], v6(osq[:]), AX.X, ALU.add), reads=['d_osq'], writes=['d_oss'])
        p.op('act', lambda e: e.activation(oss[:], oss[:], AF.Sqrt, bias=c['eps_rms'][:], scale=1.0 / 128), reads=['d_oss'], writes=['d_oss'])
        p.op('dve', lambda e: e.reciprocal(oss[:], oss[:]), reads=['d_oss'], writes=['d_oss'])
        p.op('dve', lambda e: e.tensor_tensor(v6(o0t[:]), v6(o0t[:]), bc(oss[:], [128, 6, 128], 2), ALU.mult), reads=oks + ['d_oss'], writes=oks)
        p.op('dve', lambda e: e.tensor_tensor(v6(mf[:, 0:768]), v6(o0t[:]), bc(sgb[:], [128, 6, 128], 1), ALU.mult),
             reads=oks + ['d_sgb'], writes=[(mfk, 0)])
        for k in range(8):
            p.op('pe', lambda e: e.transpose(tp8[:, k, :], mf[:, k * 128:(k + 1) * 128], c['idb'][:]),
                 reads=[(mfk, 0), (mfk, 1), 'c_idb'], writes=['d_TP'])
        p.op('act', lambda e: e.activation(mT[:], tp8, AF.Copy), reads=['d_TP'], writes=['d_mT'])
        for hh in range(2):
            for k in range(8):
                p.op('pe', lambda e: e.matmul(TPf[:], mT[:, k, :], Wout[:, k, hh * 512:(hh + 1) * 512],
                                              start=(k == 0), stop=(k == 7)),
                     reads=['d_mT', ('d_Wout', k)], writes=['d_TP'])
            p.op('dve', lambda e: e.tensor_tensor(xt[:, hh * 512:(hh + 1) * 512], TPf[:], xt[:, hh * 512:(hh + 1) * 512], ALU.add),
                 reads=['d_TP', xk], writes=[xk])
        p.dma('sp', xd[qi], xt[:], reads=[xk], writes=[('xdst', qi)])

    if _os.environ.get('DIFF_P1ONLY'):
        items = items[:6]
    emit_scores(0)
    emit_scores(1)
    for n in range(len(items)):
        if n + 2 < len(items):
            emit_scores(n + 2)
        emit_rest(n)
    p.scope_end()
    p.scope_end()


_WNAMES = ['norm_mix_g', 'norm_ffn_g', 'w_out', 'w_ff1', 'w_ff2', 'mem_norm_g', 'w_mem_kv', 'mem_q_norm_g', 'mem_k_norm_g',
           'rw_in', 'rw_w2', 'rw_a2', 'rw_g2', 'rw_lnx_g', 'rw_lnx_b', 'df_in', 'df_q_norm_g', 'df_k_norm_g',
           'df_lq1', 'df_lk1', 'df_lq2', 'df_lk2', 'df_subln_g', 'colp', 'mu']


def build_program(shapes):
    nc = bass.Bass("TRN2", target_bir_lowering=False)
    A = {}
    for n, s in shapes.items():
        A[n] = nc.dram_tensor(n, list(s), F32, kind="ExternalInput").ap()
    out = nc.dram_tensor("out", [T, D], F32, kind="ExternalOutput").ap()
    s1 = nc.dram_tensor("scr1", [T, D], F32, kind="Internal").ap()
    s2 = nc.dram_tensor("scr2", [T, D], F32, kind="Internal").ap()
    s3 = nc.dram_tensor("scr3", [T, D], F32, kind="Internal").ap()
    mo = nc.dram_tensor("scr_mo", [T, 256], BF16, kind="Internal").ap()
    qs = nc.dram_tensor("scr_q", [NT_, 128, 768], BF16, kind="Internal").ap()
    p = Prog(nc)
    c = make_consts(p)
    eps = p.sb('c_eps', [128, 1], F32)
    p.op('pool', lambda e: e.memset(eps[:], RMS_EPS), writes=['c_eps'])
    c['eps_rms'] = eps
    p.barrier()
    memkv_phase(p, c, A['mem'], A['mem_norm_g'], A['w_mem_kv'], A['mem_k_norm_g'])
    Wr = dict(rw_in=A['rw_in'], w_out=A['w_out'][0], rw_w2=A['rw_w2'], rw_a2=A['rw_a2'], rw_g2=A['rw_g2'], colp=A['colp'], mu=A['mu'],
              norm_mix_g=A['norm_mix_g'][0], rw_lnx_g=A['rw_lnx_g'], rw_lnx_b=A['rw_lnx_b'], mem_q_norm_g=A['mem_q_norm_g'][0])
    rwkv_phase(p, c, A['x'], s1, Wr)
    ffn_phase(p, c, s1, s2, A['w_ff1'][0], A['w_ff2'][0], A['norm_ffn_g'][0], False)
    Wd = dict(df_in=A['df_in'], w_out=A['w_out'][1], norm_mix_g=A['norm_mix_g'][1], mem_q_norm_g=A['mem_q_norm_g'][1],
              df_q_norm_g=A['df_q_norm_g'], df_k_norm_g=A['df_k_norm_g'], df_subln_g=A['df_subln_g'],
              df_lq1=A['df_lq1'], df_lk1=A['df_lk1'], df_lq2=A['df_lq2'], df_lk2=A['df_lk2'])
    diff_phase(p, c, s2, s3, mo, qs, Wd)
    ffn_phase(p, c, s3, out, A['w_ff1'][1], A['w_ff2'][1], A['norm_ffn_g'][1], True)
    p.finish()
    return nc


def kernel(**inputs):
    f = lambda a: np.ascontiguousarray(np.asarray(a, dtype=np.float32))
    I = {k: f(v) for k, v in inputs.items()}
    colp = np.stack([I['rw_w0'][0], I['rw_a0'][0], I['rw_k_k'][0], I['rw_k_a'][0], I['rw_r_k'][0].reshape(768)])
    colp = np.ascontiguousarray(colp.reshape(5, 6, 128).transpose(2, 0, 1))
    mu = np.ascontiguousarray(I['rw_mu'][0].reshape(20, 128).T)
    shared = dict(
        norm_mix_g=I['norm_mix_g'], norm_ffn_g=I['norm_ffn_g'], w_out=I['w_out'], w_ff1=I['w_ff1'], w_ff2=I['w_ff2'],
        mem_norm_g=I['mem_norm_g'], w_mem_kv=I['w_mem_kv'], mem_q_norm_g=I['mem_q_norm_g'], mem_k_norm_g=I['mem_k_norm_g'],
        rw_in=I['rw_in'][0], rw_w2=I['rw_w2'][0], rw_a2=I['rw_a2'][0], rw_g2=I['rw_g2'][0],
        rw_lnx_g=I['rw_lnx_g'][0], rw_lnx_b=I['rw_lnx_b'][0], df_in=I['df_in'][0],
        df_q_norm_g=f(I['df_q_norm_g'][0].reshape(128)), df_k_norm_g=f(I['df_k_norm_g'][0].reshape(128)),
        df_lq1=I['df_lq1'][0], df_lk1=I['df_lk1'][0], df_lq2=I['df_lq2'][0], df_lk2=I['df_lk2'][0],
        df_subln_g=I['df_subln_g'][0], colp=colp, mu=mu)
    shared = {k: f(v) for k, v in shared.items()}
    shapes = {k: v.shape for k, v in shared.items()}
    shapes['x'] = (T, D)
    shapes['mem'] = (256, D)
    nc = build_program(shapes)
    in_maps = []
    for b in range(NCORES):
        m = dict(shared)
        m['x'] = f(I['x'][b])
        m['mem'] = f(I['mem'][b])
        in_maps.append(m)
    res = run_bass_kernel_spmd(nc, in_maps, core_ids=list(range(NCORES)))
    return np.stack([np.asarray(res.results[b]['out'], dtype=np.float32) for b in range(NCORES)], axis=0)
```

```python
import numpy as np
import concourse.bass as bass
import concourse.mybir as mybir
from concourse.bass_utils import run_bass_kernel_spmd

F32 = mybir.dt.float32
BF16 = mybir.dt.bfloat16
AF = mybir.ActivationFunctionType
ALU = mybir.AluOpType
AX = mybir.AxisListType


class Prog:
    NDMA_SEM = 6

    def __init__(self, nc, same_engine_sync=('pool', 'act', 'dve')):
        self.nc = nc
        self.engs = {'pe': nc.tensor, 'act': nc.scalar, 'dve': nc.vector, 'pool': nc.gpsimd, 'sp': nc.sync}
        self.stack = []
        self.same_engine_sync = same_engine_sync
        self.csem = {}
        for e in ('pe', 'act', 'dve', 'pool'):
            self.csem[e] = self._enter(nc.semaphore("c_" + e))
        self.ccount = {e: 0 for e in self.csem}
        self.dsem = {}
        self.dval = {}
        self.dnext = {}
        for q in ('sp', 'pool', 'act'):
            self.dsem[q] = [self._enter(nc.semaphore("d_%s_%d" % (q, i))) for i in range(self.NDMA_SEM)]
            self.dval[q] = [0] * self.NDMA_SEM
            self.dnext[q] = 0
        self.waited = {e: {} for e in self.engs}
        self.last_write = {}
        self.reads_since = {}
        self.final_tickets = []
        self.ninstr = {e: 0 for e in self.engs}
        self.scopes = []
        self.excl = set()
        self.last_rt = None
        self.uid = 0

    def _enter(self, cm):
        v = cm.__enter__()
        self.stack.append(cm)
        return v

    def sb(self, name, shape, dtype):
        self.uid += 1
        return self._enter(self.nc.sbuf_tensor("%s_%d" % (name, self.uid), shape, dtype))

    def ps(self, name, shape, dtype):
        self.uid += 1
        return self._enter(self.nc.psum_tensor("%s_%d" % (name, self.uid), shape, dtype))

    def _wait(self, eng, ticket):
        semkey, sem, val, src = ticket
        if src == eng and eng not in self.same_engine_sync:
            return
        w = self.waited[eng]
        if w.get(semkey, 0) >= val:
            return
        w[semkey] = val
        self.engs[eng].wait_ge(sem, val)
        self.ninstr[eng] += 1

    def _deps(self, eng, reads, writes):
        for k in reads:
            t = self.last_write.get(k)
            if t is not None:
                self._wait(eng, t)
            if k in self.excl:
                for t in self.reads_since.get(k, ()):
                    if t[3] != eng:
                        self._wait(eng, t)
        for k in writes:
            t = self.last_write.get(k)
            if t is not None:
                self._wait(eng, t)
            for t in self.reads_since.get(k, ()):
                self._wait(eng, t)

    def _record(self, ticket, reads, writes):
        for k in reads:
            self.reads_since.setdefault(k, []).append(ticket)
        for k in writes:
            self.last_write[k] = ticket
            self.reads_since[k] = []

    def op(self, eng, fn, reads=(), writes=(), rt=None):
        self._deps(eng, reads, writes)
        if eng == 'pe':
            if rt != self.last_rt and self.ccount['pe'] > 0:
                w = self.waited['pe']
                if w.get('c_pe', 0) < self.ccount['pe']:
                    w['c_pe'] = self.ccount['pe']
                    self.engs['pe'].wait_ge(self.csem['pe'], self.ccount['pe'])
            self.last_rt = rt
        ins = fn(self.engs[eng])
        self.ccount[eng] += 1
        ins.then_inc(self.csem[eng], 1)
        self.ninstr[eng] += 1
        t = ('c_' + eng, self.csem[eng], self.ccount[eng], eng)
        self._record(t, reads, writes)
        return t

    def dma(self, q, out, in_, reads=(), writes=(), out_final=False, **kw):
        self._deps(q, reads, writes)
        i = self.dnext[q]
        self.dnext[q] = (i + 1) % self.NDMA_SEM
        sem = self.dsem[q][i]
        semkey = 'd_%s_%d' % (q, i)
        if self.dval[q][i] > 0:
            self._wait(q, (semkey, sem, self.dval[q][i], None))
        self.dval[q][i] += 16
        self.engs[q].dma_start(out=out, in_=in_, **kw).then_inc(sem, 16)
        self.ninstr[q] += 1
        t = (semkey, sem, self.dval[q][i], None)
        self._record(t, reads, writes)
        if out_final:
            self.final_tickets.append(t)
        return t

    def scope_begin(self):
        self.scopes.append(len(self.stack))

    def scope_end(self):
        self.barrier()
        n = self.scopes.pop()
        while len(self.stack) > n:
            self.stack.pop().__exit__(None, None, None)

    def barrier(self):
        for e in self.engs:
            for f in self.csem:
                if f != e and self.ccount[f] > 0:
                    self._wait(e, ('c_' + f, self.csem[f], self.ccount[f], f))
            for q in self.dsem:
                for i in range(self.NDMA_SEM):
                    if self.dval[q][i] > 0:
                        self._wait(e, ('d_%s_%d' % (q, i), self.dsem[q][i], self.dval[q][i], None))
        self.last_write = {}
        self.reads_since = {}

    def finish(self):
        for t in self.final_tickets:
            self._wait('sp', t)
        while self.stack:
            self.stack.pop().__exit__(None, None, None)


D = 1024
T = 4096
DFF = 4096
NCORES = 8
RMS_EPS = 1e-6


class Rot:
    def __init__(self, p, name, shape, dtype, n, psum=False):
        mk = p.ps if psum else p.sb
        self.tiles = [mk("%s%d" % (name, i), shape, dtype) for i in range(n)]
        self.keys = [(name, i) for i in range(n)]
        if psum:
            p.excl.update(self.keys)
        self.i = 0

    def next(self):
        t, k = self.tiles[self.i], self.keys[self.i]
        self.i = (self.i + 1) % len(self.tiles)
        return t, k


def make_consts(p):
    c = {}
    ones = p.sb('c_ones', [128, 128], F32)
    p.op('pool', lambda e: e.memset(ones[:], 1.0), writes=['c_ones'])
    idf = p.sb('c_idf', [128, 128], F32)
    p.op('pool', lambda e: e.affine_select(out=idf[:], in_=ones[:], pattern=[[-1, 128]],
                                           compare_op=ALU.is_equal, fill=0.0, base=0, channel_multiplier=1),
         reads=['c_ones'], writes=['c_idf'])
    idb = p.sb('c_idb', [128, 128], BF16)
    p.op('pool', lambda e: e.tensor_copy(idb[:], idf[:]), reads=['c_idf'], writes=['c_idb'])
    c['ones'], c['idf'], c['idb'] = ones, idf, idb
    return c


class NormT:
    def __init__(self, p, c, name, tp=None, nxn=1):
        self.p, self.c = p, c
        self.ss = Rot(p, name + '_ss', [128, 1], F32, 4)
        self.rs = Rot(p, name + '_rs', [128, 1], F32, 4)
        self.xn = Rot(p, name + '_xn', [128, D], BF16, nxn)
        self.tp = tp if tp is not None else Rot(p, name + '_tp', [128, 8, 128], BF16, 2, psum=True)

    def emit(self, x_ap, xkey, gb, gkey, hT, hkeys, col0, evac_eng='act'):
        st = self.emit_a(x_ap, xkey, gb, gkey)
        self.emit_b(st, hT, hkeys, col0, evac_eng)

    def emit_a(self, x_ap, xkey, gb, gkey):
        p, c = self.p, self.c
        ss, ssk = self.ss.next()
        rs, rsk = self.rs.next()
        xn, xnk = self.xn.next()
        p.op('act', lambda e: e.activation(xn[:], x_ap, AF.Square, accum_out=ss[:]),
             reads=[xkey], writes=[xnk, ssk])
        p.op('act', lambda e: e.activation(rs[:], ss[:], AF.Sqrt, bias=c['eps_rms'][:], scale=1.0 / D),
             reads=[ssk], writes=[rsk])
        p.op('dve', lambda e: e.reciprocal(rs[:], rs[:]), reads=[rsk], writes=[rsk])
        p.op('dve', lambda e: e.scalar_tensor_tensor(xn[:], x_ap, rs[:], gb, ALU.mult, ALU.mult),
             reads=[xkey, rsk, gkey], writes=[xnk])
        return xn, xnk

    def emit_b(self, st, hT, hkeys, col0, evac_eng='act'):
        p, c = self.p, self.c
        xn, xnk = st
        tp, tpk = self.tp.next()
        for k in range(8):
            p.op('pe', lambda e: e.transpose(tp[:, k, :], xn[:, k * 128:(k + 1) * 128], c['idb'][:]),
                 reads=[xnk, 'c_idb'], writes=[tpk])
        if evac_eng == 'act':
            p.op('act', lambda e: e.activation(hT[:, :, col0:col0 + 128], tp[:], AF.Copy),
                 reads=[tpk], writes=hkeys)
        else:
            p.op(evac_eng, lambda e: e.tensor_copy(hT[:, :, col0:col0 + 128], tp[:]),
                 reads=[tpk], writes=hkeys)


def load_w_bf16(p, dst, dkey, src2d, nk, ncols, split=None):
    keys = []
    for k in range(nk):
        key = (dkey, k)
        p.dma('pool', dst[:, k, :], src2d[k * 128:(k + 1) * 128, :], writes=[key], max_dma_last_dim=8192)
        keys.append(key)
    return keys


def ffn_phase(p, c, x_src, x_dst, w1, w2, gvec, is_final):
    G = 256
    NS = G // 128
    NG = T // G
    p.scope_begin()
    W1 = p.sb('W1', [128, 8, DFF], BF16)
    W2 = p.sb('W2', [128, 32, D], BF16)
    gb = p.sb('f_gb', [128, D], F32)
    p.dma('sp', gb[:], gvec.partition_broadcast(128), writes=['f_gb'])
    w1keys = load_w_bf16(p, W1, 'W1', w1, 8, DFF)
    w2keys = load_w_bf16(p, W2, 'W2', w2, 32, D)
    xg = Rot(p, 'f_xg', [128, NS, D], F32, 2)
    hTr = Rot(p, 'f_hT', [128, 8, G], BF16, 2)
    hid = p.sb('f_hid', [128, 32, G], BF16)
    sq = Rot(p, 'f_sq', [128, G], F32, 3)
    ups = Rot(p, 'f_ups', [128, G], F32, 3, psum=True)
    yps = Rot(p, 'f_yps', [128, 512], F32, 2, psum=True)
    nt = NormT(p, c, 'f_nt', nxn=NS)
    xs = x_src.rearrange("(n s q) d -> n q s d", s=NS, q=128)
    xd = x_dst.rearrange("(n s q) d -> n q s d", s=NS, q=128)

    def load_norm(g):
        xt, xk = xg.next()
        hT, hk = hTr.next()
        p.dma('sp', xt[:], xs[g], writes=[xk], reads=[('xsrc', g)])
        sts = [nt.emit_a(xt[:, s, :], xk, gb[:], 'f_gb') for s in range(NS)]
        return xt, xk, hT, hk, sts

    def transposes(nx):
        xt, xk, hT, hk, sts = nx
        for s in range(NS):
            nt.emit_b(sts[s], hT, [hk], s * 128)

    nxt = load_norm(0)
    transposes(nxt)
    for g in range(NG):
        xt, xk, hT, hk, _ = nxt
        for f in range(32):
            u, uk = ups.next()
            for k in range(8):
                p.op('pe', lambda e: e.matmul(u[:], W1[:, k, f * 128:(f + 1) * 128], hT[:, k, :],
                                              start=(k == 0), stop=(k == 7)),
                     reads=[('W1', k), hk], writes=[uk])
            s2, sk = sq.next()
            p.op('act', lambda e: e.activation(s2[:], u[:], AF.Square), reads=[uk], writes=[sk])
            p.op('dve', lambda e: e.scalar_tensor_tensor(hid[:, f, :], u[:], 0.0, s2[:], ALU.is_gt, ALU.mult),
                 reads=[uk, sk], writes=[('f_hid', f)])
        if g + 1 < NG:
            nxt = load_norm(g + 1)
        for s in range(NS):
            for h in range(2):
                y, yk = yps.next()
                for f in range(32):
                    p.op('pe', lambda e: e.matmul(y[:], hid[:, f, s * 128:(s + 1) * 128],
                                                  W2[:, f, h * 512:(h + 1) * 512],
                                                  start=(f == 0), stop=(f == 31)),
                         reads=[('f_hid', f), ('W2', f)], writes=[yk])
                p.op('dve', lambda e: e.tensor_tensor(xt[:, s, h * 512:(h + 1) * 512], y[:],
                                                      xt[:, s, h * 512:(h + 1) * 512], ALU.add),
                     reads=[yk, xk], writes=[xk])
        if g + 1 < NG:
            transposes(nxt)
        p.dma('sp', xd[g], xt[:], reads=[xk], writes=[('xdst', g)], out_final=is_final)
    p.scope_end()


def bc(ap, shape, axis):
    return ap.unsqueeze(axis).broadcast_to(shape)


def memkv_phase(p, c, mem, mem_g, w_kv, kg2):
    KmT = [p.sb('KmT%d' % l, [64, 4, 256], BF16) for l in range(2)]
    Vm = p.sb('Vm', [128, 2, 4, 65], BF16)
    c['KmT'], c['Vm'] = KmT, Vm
    p.scope_begin()
    Wkv = p.sb('m_Wkv', [128, 8, 512], BF16)
    load_w_bf16(p, Wkv, 'm_Wkv', w_kv, 8, 512)
    gb = p.sb('m_gb', [128, D], F32)
    p.dma('sp', gb[:], mem_g.partition_broadcast(128), writes=['m_gb'])
    kgb = p.sb('m_kgb', [128, 2, 64], F32)
    for l in range(2):
        p.dma('sp', kgb[:, l, :], kg2[l].partition_broadcast(128), writes=[('m_kgb', l)])
    mt_ = p.sb('m_x', [128, 2, D], F32)
    p.dma('sp', mt_[:], mem.rearrange("(s q) d -> q s d", q=128), writes=['m_x'])
    hT = p.sb('m_hT', [128, 8, 256], BF16)
    nt = NormT(p, c, 'm_nt')
    kvp = p.ps('m_kvp', [128, 512], F32)
    ktp = p.ps('m_ktp', [64, 4, 128], BF16)
    p.excl.update(['m_kvp', 'm_ktp'])
    sq = p.sb('m_sq', [128, 256], F32)
    ssum = p.sb('m_ssum', [128, 4], F32)
    kn = p.sb('m_kn', [128, 256], F32)
    kl = p.sb('m_kl', [128, 256], BF16)
    p.op('pool', lambda e: e.memset(Vm[:, :, :, 64:65], 1.0), writes=['Vm1'])
    for s in range(2):
        nt.emit(mt_[:, s, :], 'm_x', gb[:], 'm_gb', hT, ['m_hT'], s * 128)
    for s in range(2):
        for k in range(8):
            p.op('pe', lambda e: e.matmul(kvp[:], hT[:, k, s * 128:(s + 1) * 128], Wkv[:, k, :],
                                          start=(k == 0), stop=(k == 7)),
                 reads=['m_hT', ('m_Wkv', k)], writes=['m_kvp'])
        p.op('act', lambda e: e.activation(sq[:], kvp[:, 0:256], AF.Square), reads=['m_kvp'], writes=['m_sq'])
        p.op('dve', lambda e: e.tensor_reduce(ssum[:], sq[:].rearrange("p (h d) -> p h d", h=4), AX.X, ALU.add),
             reads=['m_sq'], writes=['m_ssum'])
        p.op('act', lambda e: e.activation(ssum[:], ssum[:], AF.Sqrt, bias=c['eps_rms'][:], scale=1.0 / 64),
             reads=['m_ssum'], writes=['m_ssum'])
        p.op('dve', lambda e: e.reciprocal(ssum[:], ssum[:]), reads=['m_ssum'], writes=['m_ssum'])
        p.op('dve', lambda e: e.tensor_tensor(kn[:].rearrange("p (h d) -> p h d", h=4),
                                              kvp[:, 0:256].rearrange("p (h d) -> p h d", h=4),
                                              bc(ssum[:], [128, 4, 64], 2), ALU.mult),
             reads=['m_kvp', 'm_ssum'], writes=['m_kn'])
        p.op('act', lambda e: e.activation(Vm[:, s, :, 0:64], kvp[:, 256:512].rearrange("p (h d) -> p h d", h=4),
                                           AF.Copy), reads=['m_kvp'], writes=[('Vm', s)])
        for l in range(2):
            p.op('dve', lambda e: e.tensor_tensor(kl[:].rearrange("p (h d) -> p h d", h=4),
                                                  kn[:].rearrange("p (h d) -> p h d", h=4),
                                                  bc(kgb[:, l, :], [128, 4, 64], 1), ALU.mult),
                 reads=['m_kn', ('m_kgb', l)], writes=['m_kl'])
            for h in range(4):
                p.op('pe', lambda e: e.transpose(ktp[:, h, :], kl[:, h * 64:(h + 1) * 64], c['idb'][:]),
                     reads=['m_kl', 'c_idb'], writes=['m_ktp'])
            p.op('act', lambda e: e.activation(KmT[l][:, :, s * 128:(s + 1) * 128], ktp[:], AF.Copy),
                 reads=['m_ktp'], writes=[('KmT', l, s)])
    p.scope_end()


class MemAttn:
    def __init__(self, p, c, name, qg_dram, layer, TP, tpk):
        self.p, self.c, self.l = p, c, layer
        self.n = name
        self.qgb = p.sb(name + '_qgb', [128, 64], F32)
        p.dma('sp', self.qgb[:], qg_dram.partition_broadcast(128), writes=[name + '_qgb'])
        self.sq = p.sb(name + '_sq', [128, 256], F32)
        self.ssum = p.sb(name + '_ssum', [128, 4], F32)
        self.qn = p.sb(name + '_qn', [128, 256], F32)
        self.qb = p.sb(name + '_qb', [128, 256], BF16)
        self.qT = p.sb(name + '_qT', [64, 4, 128], BF16)
        self.pT = p.sb(name + '_pT', [128, 8, 128], BF16)
        self.rc = p.sb(name + '_rc', [128, 4], F32)
        self.TP, self.tpk = TP, tpk

    def emit(self, qps, qkey, sc, sckey, oc, ockey, out_ap, outkey, extra_sc_keys=(), extra_out_keys=()):
        p, c, n, l = self.p, self.c, self.n, self.l
        KmT, Vm = c['KmT'][l], c['Vm']
        v4 = lambda a: a.rearrange("p (h d) -> p h d", h=4)
        p.op('act', lambda e: e.activation(self.sq[:], qps, AF.Square), reads=[qkey], writes=[n + 'sq'])
        p.op('dve', lambda e: e.tensor_reduce(self.ssum[:], v4(self.sq[:]), AX.X, ALU.add),
             reads=[n + 'sq'], writes=[n + 'ss'])
        p.op('act', lambda e: e.activation(self.ssum[:], self.ssum[:], AF.Sqrt, bias=c['eps_rms'][:], scale=1.0 / 64),
             reads=[n + 'ss'], writes=[n + 'ss'])
        p.op('dve', lambda e: e.reciprocal(self.ssum[:], self.ssum[:]), reads=[n + 'ss'], writes=[n + 'ss'])
        p.op('dve', lambda e: e.tensor_tensor(v4(self.qn[:]), v4(qps), bc(self.ssum[:], [128, 4, 64], 2), ALU.mult),
             reads=[qkey, n + 'ss'], writes=[n + 'qn'])
        p.op('dve', lambda e: e.tensor_tensor(v4(self.qb[:]), v4(self.qn[:]), bc(self.qgb[:], [128, 4, 64], 1), ALU.mult),
             reads=[n + 'qn', n + '_qgb'], writes=[n + 'qb'])
        tpv = self.TP[0:64, 0:512].rearrange("p (h t) -> p h t", h=4)
        for h in range(4):
            p.op('pe', lambda e: e.transpose(tpv[:, h, :], self.qb[:, h * 64:(h + 1) * 64], c['idb'][:]),
                 reads=[n + 'qb', 'c_idb'], writes=[self.tpk])
        p.op('act', lambda e: e.activation(self.qT[:], tpv, AF.Copy), reads=[self.tpk], writes=[n + 'qT'])
        for h in range(4):
            for mc in range(2):
                p.op('pe', lambda e: e.matmul(sc[:, h * 2 + mc, :], KmT[:, h, mc * 128:(mc + 1) * 128],
                                              self.qT[:, h, :], start=True, stop=True),
                     reads=[n + 'qT', ('KmT', l, mc)], writes=[sckey], rt=0)
        p.op('act', lambda e: e.activation(self.pT[:], sc, AF.Exp, scale=0.125), reads=[sckey], writes=[n + 'pT'])
        for h in range(4):
            for mc in range(2):
                p.op('pe', lambda e: e.matmul(oc[:, h, :], self.pT[:, h * 2 + mc, :], Vm[:, mc, h, :],
                                              start=(mc == 0), stop=(mc == 1)),
                     reads=[n + 'pT', ('Vm', mc), 'Vm1'], writes=[ockey])
        p.op('dve', lambda e: e.reciprocal(self.rc[:], oc[:, :, 64]), reads=[ockey], writes=[n + 'rc'])
        p.op('dve', lambda e: e.tensor_tensor(v4(out_ap), oc[:, :, 0:64], bc(self.rc[:], [128, 4, 64], 2), ALU.mult),
             reads=[ockey, n + 'rc'], writes=[outkey] + list(extra_out_keys))


C0 = 0.6065306597126334
GN_EPS = 64e-5
NT_ = T // 128
import os as _os
_STOP = _os.environ.get('RW_STOP', '')


def rwkv_phase(p, c, x_src, x_dst, W, ntiles=NT_):
    p.scope_begin()
    Win = p.sb('r_Win', [128, 8, 2816], BF16)
    load_w_bf16(p, Win, 'r_Win', W['rw_in'], 8, 2816)
    Wout = p.sb('r_Wout', [128, 8, D], BF16)
    load_w_bf16(p, Wout, 'r_Wout', W['w_out'], 8, D)
    LW = p.sb('r_LW', [128, 768], BF16)
    p.dma('pool', LW[0:64, :], W['rw_w2'], writes=['r_LWa'])
    p.dma('pool', LW[64:128, :], W['rw_a2'], writes=['r_LWb'])
    G2 = p.sb('r_G2', [128, 768], BF16)
    p.dma('pool', G2[:], W['rw_g2'], writes=['r_G2'])
    colp = p.sb('r_colp', [128, 5, 6], F32)
    p.dma('sp', colp[:], W['colp'], writes=['r_colp'])
    mu = p.sb('r_mu', [128, 20], F32)
    p.dma('sp', mu[:], W['mu'], writes=['r_mu'])
    omk = p.sb('r_omk', [128, 6], F32)
    p.op('dve', lambda e: e.tensor_scalar(omk[:], colp[:, 3, :], -1.0, 1.0, ALU.mult, ALU.add),
         reads=['r_colp'], writes=['r_omk'])
    gb = p.sb('r_gb', [128, D], F32)
    p.dma('sp', gb[:], W['norm_mix_g'].partition_broadcast(128), writes=['r_gb'])
    lng = p.sb('r_lng', [128, 768], F32)
    p.dma('sp', lng[:], W['rw_lnx_g'].partition_broadcast(128), writes=['r_lng'])
    lnb = p.sb('r_lnb', [128, 768], F32)
    p.dma('sp', lnb[:], W['rw_lnx_b'].partition_broadcast(128), writes=['r_lnb'])
    ones = c['ones']
    m_su = p.sb('r_msu', [128, 128], F32)
    p.op('pool', lambda e: e.affine_select(out=m_su[:], in_=ones[:], pattern=[[1, 128]], compare_op=ALU.is_gt,
                                           fill=0.0, base=0, channel_multiplier=-1), reads=['c_ones'], writes=['r_msu'])
    m_u = p.sb('r_mu_', [128, 128], F32)
    p.op('pool', lambda e: e.affine_select(out=m_u[:], in_=ones[:], pattern=[[1, 128]], compare_op=ALU.is_ge,
                                           fill=0.0, base=0, channel_multiplier=-1), reads=['c_ones'], writes=['r_mu_'])
    m_sl = p.sb('r_msl', [128, 128], F32)
    p.op('pool', lambda e: e.affine_select(out=m_sl[:], in_=ones[:], pattern=[[-1, 128]], compare_op=ALU.is_gt,
                                           fill=0.0, base=0, channel_multiplier=1), reads=['c_ones'], writes=['r_msl'])
    hind = p.sb('r_hind', [128, 2], F32)
    p.op('pool', lambda e: e.memset(hind[:], 0.0), writes=['r_hind'])
    p.op('pool', lambda e: e.memset(hind[0:64, 0:1], 1.0), writes=['r_hind'])
    p.op('pool', lambda e: e.memset(hind[64:128, 1:2], 1.0), writes=['r_hind'])
    bones = p.sb('r_bones', [128, 128], F32)
    p.op('pool', lambda e: e.memset(bones[:], 0.0), writes=['r_bones'])
    p.op('pool', lambda e: e.memset(bones[0:64, 0:64], 1.0), writes=['r_bones'])
    p.op('pool', lambda e: e.memset(bones[64:128, 64:128], 1.0), writes=['r_bones'])
    rmask = p.sb('r_rmask', [128, 6, 128], BF16)
    p.op('pool', lambda e: e.memset(rmask[:], 1.0), writes=['r_rmask'])
    p.op('pool', lambda e: e.memset(rmask[:, :, 0:1], 0.0), writes=['r_rmask'])
    gneps = p.sb('r_gneps', [128, 1], F32)
    p.op('pool', lambda e: e.memset(gneps[:], GN_EPS), writes=['r_gneps'])
    TP = p.ps('r_TP', [128, 1024], BF16)
    X = p.ps('r_X', [128, 3, 512], F32)
    WS = p.ps('r_WS', [128, 2, 512], F32)
    PY = p.ps('r_PY', [128, 2, 512], F32)
    p.excl.update(['r_TP', 'r_X0', 'r_X1', 'r_X2', 'r_WS', 'r_PY'])
    ws6 = WS[:].rearrange("p b (j t) -> p (b j) t", t=128)[:, 0:6, :]
    py8 = PY[:].rearrange("p b (j t) -> p (b j) t", t=128)
    tp8 = TP[:].rearrange("p (k t) -> p k t", t=128)
    x12 = X[:, 0:2, :].rearrange("p b (h v) -> p (b h) v", v=64)[:, 0:12, :]
    X2b = X[:, 2, :].bitcast(BF16)
    x2tp8 = X2b.rearrange("p (k t) -> p k t", t=128)
    XK01 = ['r_X0', 'r_X1']

    class _TPRot:
        def next(self_):
            return tp8, 'r_TP'
    nt = NormT(p, c, 'r_nt', tp=_TPRot())
    ma = MemAttn(p, c, 'r_ma', W['mem_q_norm_g'], 0, X2b, 'r_X2')
    hT = p.sb('r_hT', [128, 8, 128], BF16)
    pjs = p.sb('r_pjs', [128, 20, 129], F32)
    p.op('pool', lambda e: e.memset(pjs[:, :, 0:1], 0.0), writes=[('r_pjs', 0), ('r_pjs', 1), ('r_pjs', 2)])
    dtmp = Rot(p, 'r_dtmp', [128, 128], F32, 3)
    slab = p.sb('r_slab', [128, 20, 128], F32)

    def f32t(nm):
        return p.sb('r_' + nm, [128, 6, 128], F32)
    sg, cs, E1, E3, av, kk, k2, tmp1 = [f32t(n) for n in
        ['sg', 'cs', 'E1', 'E3', 'av', 'kk', 'k2', 'tmp1']]
    csm, E4, rkr, kkn, E2 = E1, E3, sg, kk, E1
    lora_b = p.sb('r_lorab', [128, 128], BF16)
    Bh = p.sb('r_Bh', [128, 6, 128], BF16)
    Kh = p.sb('r_Kh', [128, 6, 128], BF16)
    nbias = p.sb('r_nbias', [128, 6], F32)
    IF = []
    for par in range(2):
        d = dict(
            AR=p.sb('r_AR', [128, 6, 2, 128], BF16), Bt=p.sb('r_Bt', [128, 6, 128], BF16), Kt=p.sb('r_Kt', [128, 6, 128], BF16),
            Bht=p.sb('r_Bht', [128, 768], BF16), Kht=p.sb('r_Kht', [128, 768], BF16), V32=p.sb('r_V32', [128, 768], F32),
            Vt=p.sb('r_Vt', [128, 768], BF16), sgd=p.sb('r_sgd', [128, 128], BF16), qm=p.sb('r_qm', [128, 256], F32),
            gC=p.sb('r_gC', [128, 6], F32), bsum=p.sb('r_bsum', [128, 12], F32), xt=p.sb('r_x', [128, D], F32))
        d['k'] = lambda nm, par=par: (nm, par)
        IF.append(d)
    M32 = p.sb('r_M32', [128, 6, 64], F32)
    Mblk = p.sb('r_Mblk', [128, 6, 2, 64], BF16)
    p.op('pool', lambda e: e.memset(M32[:], 0.0), writes=['r_M32'])
    p.op('pool', lambda e: e.memset(Mblk[:], 0.0), writes=['r_Mb'])
    NTs = p.sb('r_NTs', [128, 4, 128], BF16)
    Ss = p.sb('r_Ss', [128, 4, 128], BF16)
    Ls = p.sb('r_Ls', [128, 4, 128], BF16)
    TT = p.sb('r_TT', [128, 12, 128], BF16)
    AKT = p.sb('r_AKT', [128, 12, 128], BF16)
    RBT = p.sb('r_RBT', [128, 12, 128], BF16)
    RKT = p.sb('r_RKT', [128, 12, 128], BF16)
    Wsb = p.sb('r_Wsb', [128, 768], BF16)
    Usb = p.sb('r_Usb', [128, 768], BF16)
    st1 = p.sb('r_st1', [128, 12], F32)
    st2 = p.sb('r_st2', [128, 12], F32)
    st3 = p.sb('r_st3', [128, 12], F32)
    ebuf = p.sb('r_ebuf', [128, 768], F32)
    yn = p.sb('r_yn', [128, 768], F32)
    mixf = RBT[:, 0:8, :].rearrange("p a t -> p (a t)")
    mT = AKT[:, 0:8, :]
    KMIX = [('r_RBT', g) for g in range(3)]
    KMT = [('r_AKT', g) for g in range(3)]
    v12 = lambda a: a.rearrange("p (h v) -> p h v", v=64)
    xs = x_src.rearrange("(n q) d -> n q d", q=128)
    xd = x_dst.rearrange("(n q) d -> n q d", q=128)
    allg = lambda nm: [(nm, g) for g in range(3)]

    def front(it):
        I = IF[it % 2]
        K = I['k']
        AR, Bt, Kt, Bht, Kht, V32, Vt, sgd, qm, gC, bsum, xt = (I[n] for n in
            ['AR', 'Bt', 'Kt', 'Bht', 'Kht', 'V32', 'Vt', 'sgd', 'qm', 'gC', 'bsum', 'xt'])
        xk = K('r_x')
        p.dma('sp', xt[:], xs[it], writes=[xk], reads=[('xsrc', it)])
        nt.emit(xt[:], xk, gb[:], 'r_gb', hT, ['r_hT'], 0)
        yield
        def shift(cc):
            dt_, dk_ = dtmp.next()
            p.op('dve', lambda e: e.tensor_tensor(dt_[:], pjs[:, cc, 0:128], pjs[:, cc, 1:129], ALU.subtract),
                 reads=[('r_pjs', cc // 8)], writes=[dk_])
            p.op('dve', lambda e: e.scalar_tensor_tensor(slab[:, cc, :], dt_[:], mu[:, cc:cc + 1],
                                                         pjs[:, cc, 1:129], ALU.mult, ALU.add),
                 reads=[dk_, ('r_pjs', cc // 8), 'r_mu'], writes=[('r_slab', cc)])
        SL = [('r_slab', cc) for cc in range(20)]
        SLL = [('r_slab', 18), ('r_slab', 19)]
        for r in (2, 0, 1):
            ncc = 8 if r < 2 else 4
            for q in range(ncc):
                cc = r * 8 + q
                for k in range(8):
                    p.op('pe', lambda e: e.matmul(py8[:, q, :], Win[:, k, cc * 128:(cc + 1) * 128], hT[:, k, :],
                                                  start=(k == 0), stop=(k == 7)),
                         reads=[('r_Win', k), 'r_hT'], writes=['r_PY'])
                if q % 2 == 1:
                    yield
            p.op('act', lambda e: e.activation(pjs[:, r * 8:r * 8 + ncc, 1:129], py8[:, 0:ncc, :], AF.Copy),
                 reads=['r_PY'], writes=[('r_pjs', r)])
            yield
            for q in range(ncc):
                shift(r * 8 + q)
                if q % 3 == 2:
                    yield
            if r == 2:
                p.op('act', lambda e: e.activation(lora_b[0:64, :], slab[0:64, 18, :], AF.Tanh), reads=SLL, writes=['r_lorab0'])
                p.op('act', lambda e: e.activation(lora_b[64:128, :], slab[64:128, 18, :], AF.Copy), reads=SLL, writes=['r_lorab1'])
                p.op('act', lambda e: e.activation(sgd[:], slab[:, 19, :], AF.Sigmoid), reads=SLL, writes=[K('r_sgd')])
                for j in range(6):
                    p.op('pe', lambda e: e.matmul(ws6[:, j, :], LW[0:64, j * 128:(j + 1) * 128], lora_b[0:64, :],
                                                  start=True, stop=True), reads=['r_LWa', 'r_lorab0'], writes=['r_WS'], rt=0)
                yield
                for j in range(6):
                    p.op('act', lambda e: e.activation(sg[:, j, :], ws6[:, j, :], AF.Sigmoid, bias=colp[:, 0, j:j + 1]),
                         reads=['r_WS', 'r_colp'], writes=['r_sg'])
                yield
                for j in range(6):
                    p.op('pe', lambda e: e.matmul(ws6[:, j, :], LW[64:128, j * 128:(j + 1) * 128], lora_b[64:128, :],
                                                  start=True, stop=True), reads=['r_LWb', 'r_lorab1', 'r_sg'], writes=['r_WS'], rt=64)
                yield
                for j in range(6):
                    p.op('act', lambda e: e.activation(av[:, j, :], ws6[:, j, :], AF.Sigmoid, bias=colp[:, 1, j:j + 1]),
                         reads=['r_WS', 'r_colp'], writes=['r_av'])
                yield
                p.op('dve', lambda e: e.tensor_tensor_scan(cs[:].rearrange("p j t -> p (j t)"),
                                                           rmask[:].rearrange("p j t -> p (j t)"),
                                                           sg[:].rearrange("p j t -> p (j t)"), 0.0, ALU.mult, ALU.add),
                     reads=['r_sg', 'r_rmask'], writes=['r_cs'])
                p.op('dve', lambda e: e.tensor_tensor(csm[:], cs[:], sg[:], ALU.subtract), reads=['r_cs', 'r_sg'], writes=['r_E1'])
                p.op('act', lambda e: e.activation(E1[:], csm[:], AF.Exp, scale=-C0), reads=['r_E1'], writes=['r_E1'])
                yield
                p.op('act', lambda e: e.activation(E3[:], cs[:], AF.Exp, scale=-C0), reads=['r_cs'], writes=['r_E3'])
                p.op('dve', lambda e: e.tensor_scalar(nbias[:], cs[:, :, 127], -C0, None, ALU.mult), reads=['r_cs'], writes=['r_nbias'])
                p.op('act', lambda e: e.activation(gC[:], nbias[:], AF.Exp), reads=['r_nbias'], writes=[K('r_gC')])
                yield
        for k in range(8):
            p.op('pe', lambda e: e.matmul(PY[:, 0, 0:256], hT[:, k, :], Win[:, k, 2560:2816], start=(k == 0), stop=(k == 7)),
                 reads=['r_hT', ('r_Win', k)], writes=['r_PY'])
        p.op('act', lambda e: e.activation(qm[:], PY[:, 0, 0:256], AF.Copy), reads=['r_PY'], writes=[K('r_qm')])
        p.op('act', lambda e: e.activation(pjs[:, :, 0:1], pjs[:, :, 128:129], AF.Copy),
             reads=[('r_pjs', r_) for r_ in range(3)] + SL, writes=[('r_pjs', r_) for r_ in range(3)])
        R_, K_ = slab[:, 0:6, :], slab[:, 6:12, :]
        yield
        for j in range(6):
            p.op('act', lambda e: e.activation(kk[:, j, :], slab[:, 6 + j, :], AF.Copy, scale=colp[:, 2, j:j + 1]),
                 reads=SL + ['r_colp'], writes=['r_kk'])
        p.op('act', lambda e: e.activation(tmp1[:], kk[:], AF.Square), reads=['r_kk'], writes=['r_tmp1'])
        yield
        for j in range(6):
            p.op('pe', lambda e: e.matmul(ws6[:, j, :], bones[:], tmp1[:, j, :], start=True, stop=True),
                 reads=['r_bones', 'r_tmp1', 'r_av'], writes=['r_WS'])
        yield
        p.op('act', lambda e: e.activation(tmp1[:], ws6, AF.Sqrt), reads=['r_WS'], writes=['r_tmp1'])
        p.op('dve', lambda e: e.tensor_scalar(tmp1[:], tmp1[:], 1e-12, None, ALU.max), reads=['r_tmp1'], writes=['r_tmp1'])
        p.op('dve', lambda e: e.reciprocal(tmp1[:], tmp1[:]), reads=['r_tmp1'], writes=['r_tmp1'])
        p.op('dve', lambda e: e.tensor_tensor(kkn[:], tmp1[:], kk[:], ALU.mult), reads=['r_tmp1', 'r_kk'], writes=['r_kk'])
        yield
        for j in range(6):
            p.op('act', lambda e: e.activation(tmp1[:, j, :], av[:, j, :], AF.Identity, scale=colp[:, 3, j:j + 1], bias=omk[:, j:j + 1]),
                 reads=['r_av', 'r_colp', 'r_omk', 'r_WS'], writes=['r_tmp1'])
        yield
        p.op('dve', lambda e: e.tensor_tensor(k2[:], tmp1[:], K_, ALU.mult), reads=['r_tmp1'] + SL, writes=['r_k2'])
        p.op('dve', lambda e: e.tensor_tensor(rkr[:], k2[:], R_, ALU.mult), reads=['r_k2'] + SL, writes=['r_sg'])
        for j in range(6):
            p.op('act', lambda e: e.activation(rkr[:, j, :], rkr[:, j, :], AF.Copy, scale=colp[:, 4, j:j + 1]),
                 reads=['r_sg', 'r_colp'], writes=['r_sg'])
        yield
        p.op('dve', lambda e: e.scalar_tensor_tensor(AR[:, :, 0, :], kkn[:], -1.0, E1[:], ALU.mult, ALU.mult),
             reads=['r_kk', 'r_E1'], writes=[K('r_AR0')])
        p.op('act', lambda e: e.activation(E2[:], cs[:], AF.Exp, scale=C0), reads=['r_cs'], writes=['r_E1'])
        p.op('dve', lambda e: e.tensor_tensor(AR[:, :, 1, :], R_, E3[:], ALU.mult), reads=SL + ['r_E3'], writes=[K('r_AR1')])
        yield
        for j in range(6):
            p.op('act', lambda e: e.activation(E4[:, j, :], cs[:, j, :], AF.Exp, scale=C0, bias=nbias[:, j:j + 1]),
                 reads=['r_cs', 'r_nbias'], writes=['r_E3'])
        p.op('dve', lambda e: e.tensor_tensor(tmp1[:], kkn[:], av[:], ALU.mult), reads=['r_kk', 'r_av', 'r_k2'], writes=['r_tmp1'])
        yield
        p.op('dve', lambda e: e.tensor_tensor(Bt[:], tmp1[:], E2[:], ALU.mult), reads=['r_tmp1', 'r_E1'], writes=[K('r_Bt')])
        p.op('pool', lambda e: e.tensor_tensor(Bh[:], tmp1[:], E4[:], ALU.mult), reads=['r_tmp1', 'r_E3'], writes=['r_Bh'])
        p.op('dve', lambda e: e.tensor_tensor(Kt[:], k2[:], E2[:], ALU.mult), reads=['r_k2', 'r_E1'], writes=[K('r_Kt')])
        p.op('pool', lambda e: e.tensor_tensor(Kh[:], k2[:], E4[:], ALU.mult), reads=['r_k2', 'r_E3'], writes=['r_Kh'])
        yield
        for nm, srcT, dstT in (('r_Bh', Bh, Bht), ('r_Kh', Kh, Kht)):
            for j in range(6):
                p.op('pe', lambda e: e.transpose(tp8[:, j, :], srcT[:, j, :], c['idb'][:]),
                     reads=[nm, 'c_idb'], writes=['r_TP'])
            p.op('act', lambda e: e.activation(dstT[:].rearrange("p (j t) -> p j t", t=128), tp8[:, 0:6, :], AF.Copy),
                 reads=['r_TP'], writes=[K(nm + 't')])
            yield
        for j in range(6):
            p.op('pe', lambda e: e.transpose(ws6[:, j, :], slab[:, 12 + j, :], c['idf'][:]),
                 reads=SL + ['c_idf', 'r_kk'], writes=['r_WS'])
        p.op('act', lambda e: e.activation(V32[:].rearrange("p (j t) -> p j t", t=128), ws6, AF.Copy),
             reads=['r_WS'], writes=[K('r_V32')])
        p.op('dve', lambda e: e.tensor_copy(Vt[:], V32[:]), reads=[K('r_V32')], writes=[K('r_Vt')])
        yield
        for j in range(6):
            p.op('pe', lambda e: e.matmul(WS[:, 0, j * 2:(j + 1) * 2], rkr[:, j, :], hind[:], start=True, stop=True),
                 reads=['r_sg', 'r_hind'], writes=['r_WS'])
        p.op('act', lambda e: e.activation(bsum[:], WS[:, 0, 0:12], AF.Copy), reads=['r_WS'], writes=[K('r_bsum')])
        yield

    def back(it):
        I = IF[it % 2]
        K = I['k']
        AR, Bt, Kt, Bht, Kht, V32, Vt, sgd, qm, gC, bsum, xt = (I[n] for n in
            ['AR', 'Bt', 'Kt', 'Bht', 'Kht', 'V32', 'Vt', 'sgd', 'qm', 'gC', 'bsum', 'xt'])
        xk = K('r_x')
        kAR0, kAR1, kBt, kKt = K('r_AR0'), K('r_AR1'), K('r_Bt'), K('r_Kt')
        x4 = lambda b: X[:, b, :].rearrange("p (i t) -> p i t", t=128)
        for g in range(3):
            hs = [g * 4 + i for i in range(4)]
            for i, h in enumerate(hs):
                j, P0 = h // 2, (h % 2) * 64
                sl = slice(P0, P0 + 64)
                p.op('pe', lambda e: e.matmul(X[:, 0, i * 128:(i + 1) * 128], Bt[sl, j, :], AR[sl, j, 0, :], start=True, stop=True),
                     reads=[kBt, kAR0], writes=['r_X0'], rt=P0)
                p.op('pe', lambda e: e.matmul(X[:, 1, i * 128:(i + 1) * 128], AR[sl, j, 0, :], Bt[sl, j, :], start=True, stop=True),
                     reads=[kBt, kAR0], writes=['r_X1'], rt=P0)
                p.op('pe', lambda e: e.matmul(X[:, 2, i * 128:(i + 1) * 128], Kt[sl, j, :], AR[sl, j, 0, :], start=True, stop=True),
                     reads=[kKt, kAR0], writes=['r_X2'], rt=P0)
            p.op('dve', lambda e: e.tensor_tensor(NTs[:], x4(0), bc(m_su[:], [128, 4, 128], 1), ALU.mult),
                 reads=['r_X0', 'r_msu'], writes=['r_NTs'])
            p.op('dve', lambda e: e.tensor_tensor(Ls[:], x4(1), bc(m_sl[:], [128, 4, 128], 1), ALU.mult),
                 reads=['r_X1', 'r_msl'], writes=['r_Ls'])
            p.op('dve', lambda e: e.tensor_tensor(AKT[:, g * 4:(g + 1) * 4, :], x4(2), bc(m_su[:], [128, 4, 128], 1), ALU.mult),
                 reads=['r_X2', 'r_msu'], writes=[('r_AKT', g)])
            yield
            for i, h in enumerate(hs):
                j, P0 = h // 2, (h % 2) * 64
                sl = slice(P0, P0 + 64)
                p.op('pe', lambda e: e.matmul(X[:, 0, i * 128:(i + 1) * 128], Bt[sl, j, :], AR[sl, j, 1, :], start=True, stop=True),
                     reads=[kBt, kAR1, 'r_NTs'], writes=['r_X0'], rt=P0)
                p.op('pe', lambda e: e.matmul(X[:, 2, i * 128:(i + 1) * 128], Kt[sl, j, :], AR[sl, j, 1, :], start=True, stop=True),
                     reads=[kKt, kAR1, ('r_AKT', g)], writes=['r_X2'], rt=P0)
            p.op('dve', lambda e: e.tensor_tensor(RBT[:, g * 4:(g + 1) * 4, :], x4(0), bc(m_u[:], [128, 4, 128], 1), ALU.mult),
                 reads=['r_X0', 'r_mu_'], writes=[('r_RBT', g)])
            p.op('dve', lambda e: e.tensor_tensor(RKT[:, g * 4:(g + 1) * 4, :], x4(2), bc(m_u[:], [128, 4, 128], 1), ALU.mult),
                 reads=['r_X2', 'r_mu_'], writes=[('r_RKT', g)])
            p.op('pool', lambda e: e.tensor_tensor(Ss[:], NTs[:], bc(c['idb'][:], [128, 4, 128], 1), ALU.add),
                 reads=['r_NTs', 'c_idb'], writes=['r_Ss'])
            yield
            for step in range(7):
                last = (step == 6)
                for i in range(4):
                    cs_ = slice(i * 128, (i + 1) * 128)
                    if not last:
                        p.op('pe', lambda e: e.matmul(X[:, 0, cs_], Ls[:, i, :], NTs[:, i, :], start=True, stop=True),
                             reads=['r_Ls', 'r_NTs', ('r_RBT', g)], writes=['r_X0'])
                        p.op('pe', lambda e: e.matmul(X[:, 2, cs_], NTs[:, i, :], Ls[:, i, :], start=True, stop=True),
                             reads=['r_Ls', 'r_NTs', ('r_RKT', g)], writes=['r_X2'])
                    if step > 0:
                        p.op('pe', lambda e: e.matmul(X[:, 1, cs_], Ls[:, i, :], Ss[:, i, :], start=True, stop=True),
                             reads=['r_Ls', 'r_Ss'], writes=['r_X1'])
                if step > 0:
                    dst = TT[:, g * 4:(g + 1) * 4, :] if last else Ss[:]
                    dk = ('r_TT', g) if last else 'r_Ss'
                    p.op('dve', lambda e: e.tensor_tensor(dst, x4(1), Ss[:], ALU.add), reads=['r_X1', 'r_Ss'], writes=[dk])
                if not last:
                    p.op('act', lambda e: e.activation(NTs[:], x4(0), AF.Copy), reads=['r_X0'], writes=['r_NTs'])
                    p.op('act', lambda e: e.activation(Ls[:], x4(2), AF.Copy), reads=['r_X2'], writes=['r_Ls'])
                yield
        for j in range(6):
            p.op('pe', lambda e: e.matmul(x12[:, 2 * j:2 * j + 2, :], AR[:, j, 0, :], Mblk[:, j, :, :].rearrange("p a v -> p (a v)"),
                                          start=True, stop=False),
                 reads=[kAR0, 'r_Mb'], writes=XK01)
            for h in (2 * j, 2 * j + 1):
                p.op('pe', lambda e: e.matmul(x12[:, h, :], AKT[:, h, :], Vt[:, h * 64:(h + 1) * 64], start=False, stop=(h % 2 == 1)),
                     reads=allg('r_AKT') + [K('r_Vt')], writes=XK01)
        p.op('act', lambda e: e.activation(v12(Wsb[:]), x12, AF.Copy), reads=XK01, writes=['r_Wsb'])
        yield
        for h in range(12):
            p.op('pe', lambda e: e.matmul(x12[:, h, :], TT[:, h, :], Wsb[:, h * 64:(h + 1) * 64], start=True, stop=True),
                 reads=allg('r_TT') + ['r_Wsb'], writes=XK01)
        p.op('act', lambda e: e.activation(v12(Usb[:]), x12, AF.Copy), reads=XK01, writes=['r_Usb'])
        yield
        MP = X[:, 2, 0:384].rearrange("p (j v) -> p j v", v=64)
        for h in range(12):
            j, P0 = h // 2, (h % 2) * 64
            sl = slice(P0, P0 + 64)
            hv = slice(h * 64, (h + 1) * 64)
            p.op('pe', lambda e: e.matmul(MP[sl, j, :], Bht[:, hv], Usb[:, hv], start=True, stop=False),
                 reads=[K('r_Bht'), 'r_Usb'], writes=['r_X2'])
            p.op('pe', lambda e: e.matmul(MP[sl, j, :], Kht[:, hv], Vt[:, hv], start=False, stop=True),
                 reads=[K('r_Kht'), K('r_Vt')], writes=['r_X2'])
        for j in range(6):
            p.op('pe', lambda e: e.matmul(x12[:, 2 * j:2 * j + 2, :], AR[:, j, 1, :], Mblk[:, j, :, :].rearrange("p a v -> p (a v)"),
                                          start=True, stop=False),
                 reads=[kAR1, 'r_Mb'], writes=XK01)
            for h in (2 * j, 2 * j + 1):
                hv = slice(h * 64, (h + 1) * 64)
                p.op('pe', lambda e: e.matmul(x12[:, h, :], RBT[:, h, :], Usb[:, hv], start=False, stop=False),
                     reads=allg('r_RBT') + ['r_Usb'], writes=XK01)
                p.op('pe', lambda e: e.matmul(x12[:, h, :], RKT[:, h, :], Vt[:, hv], start=False, stop=(h % 2 == 1)),
                     reads=allg('r_RKT') + [K('r_Vt')], writes=XK01)
        p.op('dve', lambda e: e.tensor_tensor(M32[:], M32[:], bc(gC[:], [128, 6, 64], 2), ALU.mult),
             reads=['r_M32', K('r_gC')], writes=['r_M32'])
        p.op('dve', lambda e: e.tensor_tensor(M32[:], M32[:], MP, ALU.add), reads=['r_M32', 'r_X2'], writes=['r_M32'])
        p.op('act', lambda e: e.activation(Mblk[0:64, :, 0, :], M32[0:64, :, :], AF.Copy), reads=['r_M32'], writes=['r_Mb'])
        p.op('act', lambda e: e.activation(Mblk[64:128, :, 1, :], M32[64:128, :, :], AF.Copy), reads=['r_M32'], writes=['r_Mb'])
        yield
        p.op('act', lambda e: e.activation(v12(ebuf[:]), x12, AF.Square), reads=XK01, writes=['r_ebuf'])
        p.op('dve', lambda e: e.tensor_reduce(st1[:], x12, AX.X, ALU.add), reads=XK01, writes=['r_st1'])
        p.op('act', lambda e: e.activation(v12(yn[:]), x12, AF.Copy), reads=XK01, writes=['r_yn'])
        p.op('dve', lambda e: e.tensor_reduce(st2[:], v12(ebuf[:]), AX.X, ALU.add), reads=['r_ebuf'], writes=['r_st2'])
        p.op('dve', lambda e: e.tensor_scalar(st1[:], st1[:], 1.0 / 64, None, ALU.mult), reads=['r_st1'], writes=['r_st1'])
        p.op('dve', lambda e: e.tensor_tensor(st3[:], st1[:], st1[:], ALU.mult), reads=['r_st1'], writes=['r_st3'])
        p.op('dve', lambda e: e.scalar_tensor_tensor(st2[:], st2[:], 1.0 / 64, st3[:], ALU.mult, ALU.subtract),
             reads=['r_st2', 'r_st3'], writes=['r_st2'])
        p.op('act', lambda e: e.activation(st2[:], st2[:], AF.Sqrt, bias=gneps[:]), reads=['r_st2', 'r_gneps'], writes=['r_st2'])
        p.op('dve', lambda e: e.reciprocal(st2[:], st2[:]), reads=['r_st2'], writes=['r_st2'])
        yield
        p.op('dve', lambda e: e.tensor_tensor(v12(yn[:]), v12(yn[:]), bc(st1[:], [128, 12, 64], 2), ALU.subtract),
             reads=['r_yn', 'r_st1'], writes=['r_yn'])
        p.op('dve', lambda e: e.tensor_tensor(v12(yn[:]), v12(yn[:]), bc(st2[:], [128, 12, 64], 2), ALU.mult),
             reads=['r_yn', 'r_st2'], writes=['r_yn'])
        p.op('dve', lambda e: e.tensor_tensor(yn[:], yn[:], lng[:], ALU.mult), reads=['r_yn', 'r_lng'], writes=['r_yn'])
        p.op('dve', lambda e: e.tensor_tensor(yn[:], yn[:], lnb[:], ALU.add), reads=['r_yn', 'r_lnb'], writes=['r_yn'])
        p.op('pool', lambda e: e.tensor_tensor(v12(ebuf[:]), v12(V32[:]), bc(bsum[:], [128, 12, 64], 2), ALU.mult),
             reads=[K('r_V32'), K('r_bsum'), 'r_ebuf'], writes=['r_ebuf'])
        p.op('dve', lambda e: e.tensor_tensor(yn[:], yn[:], ebuf[:], ALU.add), reads=['r_yn', 'r_ebuf'], writes=['r_yn'])
        yield
        for i in range(2):
            p.op('pe', lambda e: e.matmul(X[:, i, 0:384], sgd[:], G2[:, i * 384:(i + 1) * 384], start=True, stop=True),
                 reads=[K('r_sgd'), 'r_G2'], writes=['r_X%d' % i])
        for i in range(2):
            p.op('dve', lambda e: e.tensor_tensor(mixf[:, i * 384:(i + 1) * 384], X[:, i, 0:384],
                                                  yn[:, i * 384:(i + 1) * 384], ALU.mult),
                 reads=['r_X%d' % i, 'r_yn'], writes=KMIX)
        yield
        sc = X[:, 0:2, :].rearrange("p b (j t) -> p (b j) t", t=128)
        oc = X[:, 2, 0:260].rearrange("p (h e) -> p h e", e=65)
        ma.emit(qm[:], K('r_qm'), sc, 'r_X0', oc, 'r_X2', mixf[:, 768:1024], KMIX[0], extra_sc_keys=['r_X1'], extra_out_keys=KMIX[1:])
        yield
        for k in range(8):
            p.op('pe', lambda e: e.transpose(x2tp8[:, k, :], mixf[:, k * 128:(k + 1) * 128], c['idb'][:]),
                 reads=KMIX + ['c_idb'], writes=['r_X2'])
        p.op('act', lambda e: e.activation(mT, x2tp8, AF.Copy), reads=['r_X2'], writes=KMT)
        for hh in range(2):
            for k in range(8):
                p.op('pe', lambda e: e.matmul(X[:, hh, :], mT[:, k, :], Wout[:, k, hh * 512:(hh + 1) * 512],
                                              start=(k == 0), stop=(k == 7)),
                     reads=KMT + [('r_Wout', k)], writes=['r_X%d' % hh])
            p.op('dve', lambda e: e.tensor_tensor(xt[:, hh * 512:(hh + 1) * 512], X[:, hh, :],
                                                  xt[:, hh * 512:(hh + 1) * 512], ALU.add),
                 reads=['r_X%d' % hh, xk], writes=[xk])
        p.dma('sp', xd[it], xt[:], reads=[xk], writes=[('xdst', it)])
        yield

    for _ in front(0):
        pass
    for it in range(ntiles):
        gens = [back(it)]
        if it + 1 < ntiles:
            gens.append(front(it + 1))
        nb_ = int(_os.environ.get('RW_RATIO', '1'))
        while gens:
            for g_ in list(gens):
                try:
                    for _r in range(nb_ if g_ is gens[0] and len(gens) > 1 else 1):
                        next(g_)
                except StopIteration:
                    gens.remove(g_)
    p.scope_end()


import math as _math
LAMBDA_INIT = 0.8 - 0.6 * _math.exp(-0.3 * 1)
SLOPES = [2.0 ** (-8.0 * (h + 1) / 6) for h in range(6)]


def diff_phase(p, c, x_src, x_dst, mo_dram, qs_dram, W, ntiles=NT_):
    NTL = ntiles
    TL = NTL * 128
    p.scope_begin()
    qrot = Rot(p, 'd_qt', [128, 6, 128], BF16, 2)
    KT = p.sb('d_KT', [128, 6, TL], BF16)
    VA = p.sb('d_VA', [128, NTL, 6, 129], BF16)
    p.op('pool', lambda e: e.memset(VA[:, :, :, 128:129], 1.0), writes=['d_VA1'])
    TPf = p.ps('d_TP', [128, 512], F32)
    TP = TPf[:].bitcast(BF16)
    PB = p.ps('d_PB', [128, 7, 512], F32)
    p.excl.update(['d_TP'] + [('d_PB', i) for i in range(7)])
    tp8 = TP.rearrange("p (k t) -> p k t", t=128)

    class _TPRot:
        def next(self_):
            return tp8, 'd_TP'
    lq = p.sb('d_lq', [128, 4, 64], F32)
    for i, nm in enumerate(['df_lq1', 'df_lk1', 'df_lq2', 'df_lk2']):
        p.dma('sp', lq[:, i, :], W[nm].partition_broadcast(128), writes=[('d_lq', i)])
    lpr = p.sb('d_lpr', [128, 2, 64], F32)
    lsm = p.sb('d_lsm', [128, 2], F32)
    nlam = p.sb('d_nlam', [128, 1], F32)
    p.op('dve', lambda e: e.tensor_tensor(lpr[:, 0, :], lq[:, 0, :], lq[:, 1, :], ALU.mult), reads=[('d_lq', 0), ('d_lq', 1)], writes=['d_lpr0'])
    p.op('dve', lambda e: e.tensor_tensor(lpr[:, 1, :], lq[:, 2, :], lq[:, 3, :], ALU.mult), reads=[('d_lq', 2), ('d_lq', 3)], writes=['d_lpr1'])
    p.op('dve', lambda e: e.tensor_reduce(lsm[:], lpr[:], AX.X, ALU.add), reads=['d_lpr0', 'd_lpr1'], writes=['d_lsm'])
    p.op('act', lambda e: e.activation(lsm[:], lsm[:], AF.Exp), reads=['d_lsm'], writes=['d_lsm'])
    p.op('dve', lambda e: e.tensor_tensor(nlam[:], lsm[:, 1:2], lsm[:, 0:1], ALU.subtract), reads=['d_lsm'], writes=['d_nlam'])
    p.op('dve', lambda e: e.tensor_scalar(nlam[:], nlam[:], -LAMBDA_INIT, None, ALU.add), reads=['d_nlam'], writes=['d_nlam'])
    tabi = p.sb('d_tabi', [128, 32], F32)
    p.op('pool', lambda e: e.iota(tabi[:], pattern=[[128, 32]], base=-127 - 128 * 31, channel_multiplier=1,
                                  allow_small_or_imprecise_dtypes=True), writes=['d_tabi'])
    tab = p.sb('d_tab', [128, 6, 32], F32)
    for h in range(6):
        p.op('pool', lambda e: e.tensor_scalar(tab[:, h, :], tabi[:], SLOPES[h], None, ALU.mult), reads=['d_tabi'], writes=['d_tab'])
    m_u = p.sb('d_mu', [128, 128], BF16)
    p.op('pool', lambda e: e.affine_select(out=m_u[:], in_=c['ones'][:], pattern=[[1, 128]], compare_op=ALU.is_ge,
                                           fill=0.0, base=0, channel_multiplier=-1), reads=['c_ones'], writes=['d_mu'])
    p.scope_begin()
    Win = p.sb('d_Win', [128, 8, 2560], BF16)
    load_w_bf16(p, Win, 'd_Win', W['df_in'], 8, 2560)
    gb = p.sb('d_gb', [128, D], F32)
    p.dma('sp', gb[:], W['norm_mix_g'].partition_broadcast(128), writes=['d_gb'])
    qkg = p.sb('d_qkg', [128, 2, 128], F32)
    p.dma('sp', qkg[:, 0, :], W['df_q_norm_g'].partition_broadcast(128), writes=[('d_qkg', 0)])
    p.dma('sp', qkg[:, 1, :], W['df_k_norm_g'].partition_broadcast(128), writes=[('d_qkg', 1)])
    nt = NormT(p, c, 'd_nt', tp=_TPRot())
    ma = MemAttn(p, c, 'd_ma', W['mem_q_norm_g'], 1, TP, 'd_TP')
    xg = Rot(p, 'd_x', [128, D], F32, 1)
    hT = p.sb('d_hT', [128, 8, 128], BF16)
    sq = p.sb('d_sq', [128, 768], F32)
    ssm = p.sb('d_ssm', [128, 12], F32)
    qn = sq
    qb = p.sb('d_qb', [128, 768], BF16)
    mo = Rot(p, 'd_mo', [128, 256], BF16, 2)
    xs = x_src.rearrange("(n q) d -> n q d", q=128)
    xd = x_dst.rearrange("(n q) d -> n q d", q=128)
    mod = mo_dram.rearrange("(n q) d -> n q d", q=128)
    v6 = lambda a: a.rearrange("p (h e) -> p h e", h=6)
    v12 = lambda a: a.rearrange("p (g d) -> p g d", d=64)
    qmr = Rot(p, 'd_qm', [128, 256], F32, 2)

    def proj(b):
        c0 = b * 384 if b < 6 else 2304
        n = 384 if b < 6 else 256
        for k in range(8):
            p.op('pe', lambda e: e.matmul(PB[:, b, 0:n], hT[:, k, :], Win[:, k, c0:c0 + n], start=(k == 0), stop=(k == 7)),
                 reads=['d_hT', ('d_Win', k)], writes=[('d_PB', b)])

    def normproj_qk(it):
        xt, xk = xg.next()
        p.dma('sp', xt[:], xs[it], writes=[xk], reads=[('xsrc', it)])
        nt.emit(xt[:], xk, gb[:], 'd_gb', hT, ['d_hT'], 0)
        for b in range(4):
            proj(b)

    def proj_vqm(it):
        for b in range(4, 7):
            proj(b)

    def post(it):
        qtl, qtk = qrot.next()
        for qi, (dst, b0) in enumerate(((None, 0), (KT, 2))):
            for i in range(2):
                p.op('act', lambda e: e.activation(sq[:, i * 384:(i + 1) * 384], PB[:, b0 + i, 0:384], AF.Square),
                     reads=[('d_PB', b0 + i)], writes=['d_sq'])
            p.op('dve', lambda e: e.tensor_reduce(ssm[:], v12(sq[:]), AX.X, ALU.add), reads=['d_sq'], writes=['d_ssm'])
            p.op('act', lambda e: e.activation(ssm[:], ssm[:], AF.Sqrt, bias=c['eps_rms'][:], scale=1.0 / 64), reads=['d_ssm'], writes=['d_ssm'])
            p.op('dve', lambda e: e.reciprocal(ssm[:], ssm[:]), reads=['d_ssm'], writes=['d_ssm'])
            for i in range(2):
                p.op('dve', lambda e: e.tensor_tensor(v12(qn[:, i * 384:(i + 1) * 384]), v12(PB[:, b0 + i, 0:384]),
                                                      bc(ssm[:, i * 6:(i + 1) * 6], [128, 6, 64], 2), ALU.mult),
                     reads=[('d_PB', b0 + i), 'd_ssm'], writes=['d_sq'])
            p.op('dve', lambda e: e.tensor_tensor(v6(qb[:]), v6(qn[:]), bc(qkg[:, qi, :], [128, 6, 128], 1), ALU.mult),
                 reads=['d_sq', ('d_qkg', qi)], writes=['d_qb'])
            for h in range(6):
                p.op('pe', lambda e: e.transpose(tp8[:, h, :], qb[:, h * 128:(h + 1) * 128], c['idb'][:]),
                     reads=['d_qb', 'c_idb'], writes=['d_TP'])
            if qi == 0:
                p.op('act', lambda e: e.activation(qtl[:], tp8[:, 0:6, :], AF.Copy), reads=['d_TP'], writes=[qtk])
                p.dma('sp', qs_dram[it], qtl[:].rearrange("p h t -> p (h t)"), reads=[qtk], writes=[('d_QK', 0, it)])
            else:
                p.op('act', lambda e: e.activation(dst[:, :, it * 128:(it + 1) * 128], tp8[:, 0:6, :], AF.Copy),
                     reads=['d_TP'], writes=[('d_QK', qi, it)])
        for i in range(2):
            p.op('act', lambda e: e.activation(VA[:, it, i * 3:(i + 1) * 3, 0:128],
                                               PB[:, 4 + i, 0:384].rearrange("p (h e) -> p h e", h=3), AF.Copy),
                 reads=[('d_PB', 4 + i)], writes=[('d_VA', it, i)])
        qmt, qmk = qmr.next()
        p.op('act', lambda e: e.activation(qmt[:], PB[:, 6, 0:256], AF.Copy), reads=[('d_PB', 6)], writes=[qmk])
        return qmt, qmk

    def memattn(it, qmt, qmk):
        mt, mk = mo.next()
        sc = PB[:, 4:6, :].rearrange("p b (j t) -> p (b j) t", t=128)
        oc = PB[:, 6, 0:260].rearrange("p (h e) -> p h e", e=65)
        ma.emit(qmt[:], qmk, sc, ('d_PB', 4), oc, ('d_PB', 6), mt[:], mk, extra_sc_keys=[('d_PB', 5)])
        p.dma('sp', mod[it], mt[:], reads=[mk], writes=[('mo', it)])

    normproj_qk(0)
    proj_vqm(0)
    for it in range(NTL):
        qmt, qmk = post(it)
        if it + 1 < NTL:
            normproj_qk(it + 1)
        memattn(it, qmt, qmk)
        if it + 1 < NTL:
            proj_vqm(it + 1)
    p.scope_end()
    p.scope_begin()
    Wout = p.sb('d_Wout', [128, 8, D], BF16)
    load_w_bf16(p, Wout, 'd_Wout', W['w_out'], 8, D)
    sgb = p.sb('d_sgb', [128, 128], F32)
    p.dma('sp', sgb[:], W['df_subln_g'].partition_broadcast(128), writes=['d_sgb'])
    p.op('dve', lambda e: e.tensor_scalar(sgb[:], sgb[:], 1.0 - LAMBDA_INIT, None, ALU.mult), reads=['d_sgb'], writes=['d_sgb'])
    xg = Rot(p, 'd_x2', [128, D], F32, 2)
    qblk = Rot(p, 'd_qblk', [128, 6, 2, 128], BF16, 2)
    for i in range(2):
        p.op('pool', lambda e: e.memset(qblk.tiles[i][:], 0.0), writes=[qblk.keys[i]])
    pT = Rot(p, 'd_pT', [128, 2, 128], BF16, 6)
    mixf = Rot(p, 'd_mixf', [128, D], BF16, 2)
    mT = p.sb('d_mT', [128, 8, 128], BF16)
    o0 = Rot(p, 'd_o0', [128, 768], F32, 2)
    o1 = Rot(p, 'd_o1', [128, 128], F32, 2)
    rc = Rot(p, 'd_rc', [128, 2], F32, 2)
    osq = p.sb('d_osq', [128, 768], F32)
    oss = p.sb('d_oss', [128, 6], F32)
    SB = [0, 1, 6]
    items = [(qi, h, kj) for qi in range(NTL) for h in range(6) for kj in range(qi + 1)]
    state = {}

    def tile_begin(qi):
        xt, xk = xg.next()
        p.dma('sp', xt[:], xs[qi], writes=[xk], reads=[('xsrc', qi)])
        mf, mfk = mixf.next()
        p.dma('sp', mf[:, 768:1024], mod[qi], reads=[('mo', qi)], writes=[(mfk, 1)])
        qb_, qbk = qblk.next()
        qsrc = qs_dram[qi].rearrange("p (h t) -> p h t", h=6)
        p.dma('sp', qb_[0:64, :, 0, :], qsrc[0:64], reads=[('d_QK', 0, qi)], writes=[qbk])
        p.dma('sp', qb_[64:128, :, 1, :], qsrc[64:128], reads=[('d_QK', 0, qi)], writes=[qbk])
        o0t, o0k = o0.next()
        state[qi] = dict(xt=xt, xk=xk, mf=mf, mfk=mfk, qb=qb_, qbk=qbk, o0=o0t, o0k=o0k)

    def emit_scores(n):
        qi, h, kj = items[n]
        if h == 0 and kj == 0:
            tile_begin(qi)
        st = state[qi]
        sb_ = SB[n % 3]
        p.op('pe', lambda e: e.matmul(PB[:, sb_, 0:256], KT[:, h, kj * 128:(kj + 1) * 128],
                                      st['qb'][:, h, :, :].rearrange("p c t -> p (c t)"), start=True, stop=True),
             reads=[('d_QK', 1, kj), st['qbk']], writes=[('d_PB', sb_)])

    def emit_rest(n):
        qi, h, kj = items[n]
        st = state[qi]
        sb_ = SB[n % 3]
        ob = 2 if h % 2 == 0 else 4
        pt, pk = pT.next()
        p.op('act', lambda e: e.activation(pt[:].rearrange("p c t -> p (c t)"), PB[:, sb_, 0:256], AF.Exp, scale=0.125,
                                           bias=tab[:, h, 31 + kj - qi:32 + kj - qi]),
             reads=[('d_PB', sb_), 'd_tab'], writes=[pk])
        if kj == qi:
            p.op('dve', lambda e: e.tensor_tensor(pt[:], pt[:], bc(m_u[:], [128, 2, 128], 1), ALU.mult),
                 reads=[pk, 'd_mu'], writes=[pk])
        for cm in range(2):
            p.op('pe', lambda e: e.matmul(PB[:, ob + cm, 0:129], pt[:, cm, :], VA[:, kj, h, :], start=(kj == 0), stop=(kj == qi)),
                 reads=[pk, ('d_VA', kj, 0), ('d_VA', kj, 1), 'd_VA1'], writes=[('d_PB', ob + cm)])
        if kj == qi:
            head_end(qi, h, ob)
            if h == 5:
                tile_end(qi)

    def head_end(qi, h, ob):
        st = state[qi]
        o0t, o0k = st['o0'], st['o0k']
        r_, rk = rc.next()
        o1t, o1k = o1.next()
        p.op('dve', lambda e: e.reciprocal(r_[:, 0:1], PB[:, ob, 128:129]), reads=[('d_PB', ob)], writes=[rk])
        p.op('dve', lambda e: e.reciprocal(r_[:, 1:2], PB[:, ob + 1, 128:129]), reads=[('d_PB', ob + 1)], writes=[rk])
        p.op('dve', lambda e: e.tensor_scalar(r_[:, 1:2], r_[:, 1:2], nlam[:], None, ALU.mult), reads=[rk, 'd_nlam'], writes=[rk])
        p.op('dve', lambda e: e.tensor_scalar(o1t[:], PB[:, ob + 1, 0:128], r_[:, 1:2], None, ALU.mult),
             reads=[('d_PB', ob + 1), rk], writes=[o1k])
        p.op('dve', lambda e: e.scalar_tensor_tensor(o0t[:, h * 128:(h + 1) * 128], PB[:, ob, 0:128], r_[:, 0:1], o1t[:], ALU.mult, ALU.add),
             reads=[('d_PB', ob), rk, o1k], writes=[(o0k, h)])

    def tile_end(qi):
        st = state.pop(qi)
        xt, xk, mf, mfk, o0t, o0k = st['xt'], st['xk'], st['mf'], st['mfk'], st['o0'], st['o0k']
        oks = [(o0k, h) for h in range(6)]
        p.op('act', lambda e: e.activation(osq[:], o0t[:], AF.Square), reads=oks, writes=['d_osq'])
        p.op('dve', lambda e: e.tensor_reduce(oss[:], v6(osq[:]), AX.X, ALU.add), reads=['d_osq'], writes=['d_oss'])
        p.op('act', lambda e: e.activation(oss[:], oss[:], AF.Sqrt, bias=c['eps_rms'][:], scale=1.0 / 128), reads=['d_oss'], writes=['d_oss'])
        p.op('dve', lambda e: e.reciprocal(oss[:], oss[:]), reads=['d_oss'], writes=['d_oss'])
        p.op('dve', lambda e: e.tensor_tensor(v6(o0t[:]), v6(o0t[:]), bc(oss[:], [128, 6, 128], 2), ALU.mult), reads=oks + ['d_oss'], writes=oks)
        p.op('dve', lambda e: e.tensor_tensor(v6(mf[:, 0:768]), v6(o0t[:]), bc(sgb[:], [128, 6, 128], 1), ALU.mult),
             reads=oks + ['d_sgb'], writes=[(mfk, 0)])
        for k in range(8):
            p.op('pe', lambda e: e.transpose(tp8[:, k, :], mf[:, k * 128:(k + 1) * 128], c['idb'][:]),
                 reads=[(mfk, 0), (mfk, 1), 'c_idb'], writes=['d_TP'])
        p.op('act', lambda e: e.activation(mT[:], tp8, AF.Copy), reads=['d_TP'], writes=['d_mT'])
        for hh in range(2):
            for k in range(8):
                p.op('pe', lambda e: e.matmul(TPf[:], mT[:, k, :], Wout[:, k, hh * 512:(hh + 1) * 512],
                                              start=(k == 0), stop=(k == 7)),
                     reads=['d_mT', ('d_Wout', k)], writes=['d_TP'])
            p.op('dve', lambda e: e.tensor_tensor(xt[:, hh * 512:(hh + 1) * 512], TPf[:], xt[:, hh * 512:(hh + 1) * 512], ALU.add),
                 reads=['d_TP', xk], writes=[xk])
        p.dma('sp', xd[qi], xt[:], reads=[xk], writes=[('xdst', qi)])

    if _os.environ.get('DIFF_P1ONLY'):
        items = items[:6]
    emit_scores(0)
    emit_scores(1)
    for n in range(len(items)):
        if n + 2 < len(items):
            emit_scores(n + 2)
        emit_rest(n)
    p.scope_end()
    p.scope_end()


_WNAMES = ['norm_mix_g', 'norm_ffn_g', 'w_out', 'w_ff1', 'w_ff2', 'mem_norm_g', 'w_mem_kv', 'mem_q_norm_g', 'mem_k_norm_g',
           'rw_in', 'rw_w2', 'rw_a2', 'rw_g2', 'rw_lnx_g', 'rw_lnx_b', 'df_in', 'df_q_norm_g', 'df_k_norm_g',
           'df_lq1', 'df_lk1', 'df_lq2', 'df_lk2', 'df_subln_g', 'colp', 'mu']


def build_program(shapes):
    nc = bass.Bass("TRN2", target_bir_lowering=False)
    A = {}
    for n, s in shapes.items():
        A[n] = nc.dram_tensor(n, list(s), F32, kind="ExternalInput").ap()
    out = nc.dram_tensor("out", [T, D], F32, kind="ExternalOutput").ap()
    s1 = nc.dram_tensor("scr1", [T, D], F32, kind="Internal").ap()
    s2 = nc.dram_tensor("scr2", [T, D], F32, kind="Internal").ap()
    s3 = nc.dram_tensor("scr3", [T, D], F32, kind="Internal").ap()
    mo = nc.dram_tensor("scr_mo", [T, 256], BF16, kind="Internal").ap()
    qs = nc.dram_tensor("scr_q", [NT_, 128, 768], BF16, kind="Internal").ap()
    p = Prog(nc)
    c = make_consts(p)
    eps = p.sb('c_eps', [128, 1], F32)
    p.op('pool', lambda e: e.memset(eps[:], RMS_EPS), writes=['c_eps'])
    c['eps_rms'] = eps
    p.barrier()
    memkv_phase(p, c, A['mem'], A['mem_norm_g'], A['w_mem_kv'], A['mem_k_norm_g'])
    Wr = dict(rw_in=A['rw_in'], w_out=A['w_out'][0], rw_w2=A['rw_w2'], rw_a2=A['rw_a2'], rw_g2=A['rw_g2'], colp=A['colp'], mu=A['mu'],
              norm_mix_g=A['norm_mix_g'][0], rw_lnx_g=A['rw_lnx_g'], rw_lnx_b=A['rw_lnx_b'], mem_q_norm_g=A['mem_q_norm_g'][0])
    rwkv_phase(p, c, A['x'], s1, Wr)
    ffn_phase(p, c, s1, s2, A['w_ff1'][0], A['w_ff2'][0], A['norm_ffn_g'][0], False)
    Wd = dict(df_in=A['df_in'], w_out=A['w_out'][1], norm_mix_g=A['norm_mix_g'][1], mem_q_norm_g=A['mem_q_norm_g'][1],
              df_q_norm_g=A['df_q_norm_g'], df_k_norm_g=A['df_k_norm_g'], df_subln_g=A['df_subln_g'],
              df_lq1=A['df_lq1'], df_lk1=A['df_lk1'], df_lq2=A['df_lq2'], df_lk2=A['df_lk2'])
    diff_phase(p, c, s2, s3, mo, qs, Wd)
    ffn_phase(p, c, s3, out, A['w_ff1'][1], A['w_ff2'][1], A['norm_ffn_g'][1], True)
    p.finish()
    return nc


def kernel(**inputs):
    f = lambda a: np.ascontiguousarray(np.asarray(a, dtype=np.float32))
    I = {k: f(v) for k, v in inputs.items()}
    colp = np.stack([I['rw_w0'][0], I['rw_a0'][0], I['rw_k_k'][0], I['rw_k_a'][0], I['rw_r_k'][0].reshape(768)])
    colp = np.ascontiguousarray(colp.reshape(5, 6, 128).transpose(2, 0, 1))
    mu = np.ascontiguousarray(I['rw_mu'][0].reshape(20, 128).T)
    shared = dict(
        norm_mix_g=I['norm_mix_g'], norm_ffn_g=I['norm_ffn_g'], w_out=I['w_out'], w_ff1=I['w_ff1'], w_ff2=I['w_ff2'],
        mem_norm_g=I['mem_norm_g'], w_mem_kv=I['w_mem_kv'], mem_q_norm_g=I['mem_q_norm_g'], mem_k_norm_g=I['mem_k_norm_g'],
        rw_in=I['rw_in'][0], rw_w2=I['rw_w2'][0], rw_a2=I['rw_a2'][0], rw_g2=I['rw_g2'][0],
        rw_lnx_g=I['rw_lnx_g'][0], rw_lnx_b=I['rw_lnx_b'][0], df_in=I['df_in'][0],
        df_q_norm_g=f(I['df_q_norm_g'][0].reshape(128)), df_k_norm_g=f(I['df_k_norm_g'][0].reshape(128)),
        df_lq1=I['df_lq1'][0], df_lk1=I['df_lk1'][0], df_lq2=I['df_lq2'][0], df_lk2=I['df_lk2'][0],
        df_subln_g=I['df_subln_g'][0], colp=colp, mu=mu)
    shared = {k: f(v) for k, v in shared.items()}
    shapes = {k: v.shape for k, v in shared.items()}
    shapes['x'] = (T, D)
    shapes['mem'] = (256, D)
    nc = build_program(shapes)
    in_maps = []
    for b in range(NCORES):
        m = dict(shared)
        m['x'] = f(I['x'][b])
        m['mem'] = f(I['mem'][b])
        in_maps.append(m)
    res = run_bass_kernel_spmd(nc, in_maps, core_ids=list(range(NCORES)))
    return np.stack([np.asarray(res.results[b]['out'], dtype=np.float32) for b in range(NCORES)], axis=0)
```
